# Optimizing a Trainium2 kernel written in Bass

```python
import jax, jax.numpy as jnp
from jax import lax
import numpy as np

D_MODEL = 2048
BATCH = 2
SEQ = 8192
DEPTH = 2

N_MIXERS = 2
EPS = 1e-6
CONV_WIDTH = 3
CONV_DIM = D_MODEL
GLA_HEADS = 4
GLA_KEY_DIM = D_MODEL // 2
GLA_VAL_DIM = D_MODEL
GLA_HEAD_K = GLA_KEY_DIM // GLA_HEADS
GLA_HEAD_V = GLA_VAL_DIM // GLA_HEADS
GLA_GATE_RANK = 16
GATE_LOGIT_NORMALIZER = 16.0
GLA_CHUNK = 64
GLA_IN_DIM = 2 * GLA_KEY_DIM + 2 * GLA_VAL_DIM + GLA_GATE_RANK

kernel_name = "hybrid_shortconv_gla_interleaved"


def rmsnorm(x, w):
    x32 = x.astype(jnp.float32)
    y = x32 * lax.rsqrt(jnp.mean(x32 * x32, axis=-1, keepdims=True) + EPS)
    return (y * w.astype(jnp.float32)).astype(x.dtype)


def short_conv_mixer(h, w_in, w_conv, w_out):
    T = h.shape[1]
    proj = h @ w_in
    b_gate, c_gate, u, z = jnp.split(proj, 4, axis=-1)
    cu = c_gate * u
    pad = jnp.pad(cu, ((0, 0), (CONV_WIDTH - 1, 0), (0, 0)))
    conv = pad[:, 0:T, :] * w_conv[:, 0]
    for k in range(1, CONV_WIDTH):
        conv = conv + pad[:, k:k + T, :] * w_conv[:, k]
    y = b_gate * conv * jax.nn.silu(z)
    return y @ w_out


def _to_chunks(t, d):
    Bsz, T, _ = t.shape
    t = t.reshape(Bsz, T // GLA_CHUNK, GLA_CHUNK, GLA_HEADS, d)
    return jnp.transpose(t, (1, 0, 3, 2, 4))


def gla_mixer(h, w_in, w_gk2, b_gk2, gn_w, w_out):
    Bsz, T, _ = h.shape
    dtype = h.dtype
    proj = h @ w_in
    s1, s2, s3, s4 = GLA_KEY_DIM, 2 * GLA_KEY_DIM, 2 * GLA_KEY_DIM + GLA_VAL_DIM, 2 * GLA_KEY_DIM + 2 * GLA_VAL_DIM
    q, k, v, g, gk_lr = proj[..., :s1], proj[..., s1:s2], proj[..., s2:s3], proj[..., s3:s4], proj[..., s4:]
    log_alpha = jax.nn.log_sigmoid((gk_lr @ w_gk2 + b_gk2).astype(jnp.float32)) / GATE_LOGIT_NORMALIZER

    qc = _to_chunks(q.astype(jnp.float32) * (GLA_HEAD_K ** -0.5), GLA_HEAD_K)
    kc = _to_chunks(k.astype(jnp.float32), GLA_HEAD_K)
    vc = _to_chunks(v.astype(jnp.float32), GLA_HEAD_V)
    bc = jnp.cumsum(_to_chunks(log_alpha, GLA_HEAD_K), axis=3)
    causal = jnp.tril(jnp.ones((GLA_CHUNK, GLA_CHUNK), dtype=bool))

    def step(S, inp):
        qi, ki, vi, bi = inp
        inter = jnp.einsum('bhid,bhde->bhie', qi * jnp.exp(bi), S)
        diff = bi[:, :, :, None, :] - bi[:, :, None, :, :]
        decay = jnp.where(causal[:, :, None], jnp.exp(jnp.minimum(diff, 0.0)), 0.0)
        scores = jnp.einsum('bhid,bhjd,bhijd->bhij', qi, ki, decay)
        intra = jnp.einsum('bhij,bhje->bhie', scores, vi)
        b_last = bi[:, :, -1:, :]
        S_new = jnp.exp(b_last)[:, :, 0, :, None] * S + jnp.einsum('bhjd,bhje->bhde', ki * jnp.exp(b_last - bi), vi)
        return S_new, inter + intra

    S0 = jnp.zeros((Bsz, GLA_HEADS, GLA_HEAD_K, GLA_HEAD_V), jnp.float32)
    _, o = lax.scan(step, S0, (qc, kc, vc, bc))
    o = jnp.transpose(o, (1, 0, 3, 2, 4)).reshape(Bsz, T, GLA_HEADS, GLA_HEAD_V)
    o = o * lax.rsqrt(jnp.mean(o * o, axis=-1, keepdims=True) + EPS) * gn_w.astype(jnp.float32)
    o = o.reshape(Bsz, T, GLA_VAL_DIM).astype(dtype) * jax.nn.silu(g)
    return o @ w_out


def setup_inputs(seed: int = 0) -> dict:
    key = jax.random.key(seed)
    ks = jax.random.split(key, 16)
    nrm = lambda k, shape, s: jax.random.normal(k, shape, jnp.float32) * s
    return {
        "x": nrm(ks[0], (BATCH, SEQ, D_MODEL), 1.0),
        "norm0_w": 1.0 + nrm(ks[1], (D_MODEL,), 0.02),
        "conv0_w_in": nrm(ks[2], (D_MODEL, 4 * CONV_DIM), D_MODEL ** -0.5),
        "conv0_w_conv": nrm(ks[3], (CONV_DIM, CONV_WIDTH), CONV_WIDTH ** -0.5),
        "conv0_w_out": nrm(ks[4], (CONV_DIM, D_MODEL), CONV_DIM ** -0.5),
        "norm1_w": 1.0 + nrm(ks[5], (D_MODEL,), 0.02),
        "gla1_w_in": nrm(ks[6], (D_MODEL, GLA_IN_DIM), D_MODEL ** -0.5),
        "gla1_w_gk2": nrm(ks[7], (GLA_GATE_RANK, GLA_KEY_DIM), GLA_GATE_RANK ** -0.5),
        "gla1_b_gk2": nrm(ks[8], (GLA_KEY_DIM,), 0.01),
        "gla1_gn_w": 1.0 + nrm(ks[9], (GLA_HEAD_V,), 0.02),
        "gla1_w_out": nrm(ks[10], (GLA_VAL_DIM, D_MODEL), GLA_VAL_DIM ** -0.5),
        "norm_f_w": 1.0 + nrm(ks[11], (D_MODEL,), 0.02),
    }


def reference(x, norm0_w, conv0_w_in, conv0_w_conv, conv0_w_out, norm1_w, gla1_w_in, gla1_w_gk2, gla1_b_gk2, gla1_gn_w, gla1_w_out, norm_f_w):
    norms = [norm0_w, norm1_w]
    mixers = [
        lambda h: short_conv_mixer(h, conv0_w_in, conv0_w_conv, conv0_w_out),
        lambda h: gla_mixer(h, gla1_w_in, gla1_w_gk2, gla1_b_gk2, gla1_gn_w, gla1_w_out),
    ]
    for i in range(DEPTH):
        x = x + mixers[i % N_MIXERS](rmsnorm(x, norms[i]))
    return rmsnorm(x, norm_f_w)
```

```python
from contextlib import ExitStack

import numpy as np
import ml_dtypes

import concourse.bass as bass
import concourse.mybir as mybir
from concourse.bass_utils import run_bass_kernel_spmd

F32 = mybir.dt.float32
BF16 = mybir.dt.bfloat16
AF = mybir.ActivationFunctionType
ALU = mybir.AluOpType

D = 2048
SEQ = 8192
NB = 2
NCORES = 8
TOK = 2048
EPS = 1e-6
KT = 16
HK = 256
HV = 512


class Prog:
    ENG = ("pe", "act", "dve", "pool", "sp")

    def __init__(self, nc, es):
        self.nc, self.es = nc, es
        self.ops = {e: [] for e in self.ENG}
        self.sem = {}
        self.cnt = {}
        self.shared = set()
        self.waited = {e: {} for e in self.ENG}
        self.buf = {}

    def _sem(self, name):
        if name not in self.sem:
            self.sem[name] = self.es.enter_context(self.nc.semaphore(name))
            self.cnt[name] = 0
        return self.sem[name]

    def _deps(self, eng, reads, writes):
        ev = {}

        def add(n, v):
            if n in self.shared:
                v = self.cnt[n]
            if ev.get(n, 0) < v:
                ev[n] = v

        for k in reads:
            b = self.buf.get(k)
            if b and b["w"]:
                add(*b["w"])
        for k in writes:
            b = self.buf.get(k)
            if b:
                if b["w"]:
                    add(*b["w"])
                for n, v in b["r"].items():
                    add(n, v)
        out = []
        for n, v in ev.items():
            if eng == "pe" and n == "pe":
                continue
            if self.waited[eng].get(n, 0) < v:
                self.waited[eng][n] = v
                out.append((self.sem[n], v))
        return out

    def _commit(self, event, reads, writes):
        n, v = event
        for k in reads:
            b = self.buf.setdefault(k, {"w": None, "r": {}})
            if b["r"].get(n, 0) < v:
                b["r"][n] = v
        for k in writes:
            self.buf[k] = {"w": event, "r": {}}

    def op(self, eng, fn, reads=(), writes=()):
        waits = self._deps(eng, reads, writes)
        sem = self._sem(eng)
        self.cnt[eng] += 1
        val = self.cnt[eng]

        def run(e, waits=waits, fn=fn, sem=sem):
            for s, v in waits:
                e.wait_ge(s, v)
            fn(e).then_inc(sem, 1)

        self.ops[eng].append(run)
        self._commit((eng, val), reads, writes)

    def dma(self, eng, out, in_, reads=(), writes=(), sem="dma", shared=False):
        self.dmas(eng, [(out, in_)], reads, writes, sem, shared)

    def dmas(self, eng, pairs, reads=(), writes=(), sem="dma", shared=False):
        waits = self._deps(eng, reads, writes)
        name = "d_" + sem
        s = self._sem(name)
        if shared:
            self.shared.add(name)
        self.cnt[name] += 16 * len(pairs)
        val = self.cnt[name]

        def run(e, waits=waits, pairs=pairs, s=s):
            for ws, v in waits:
                e.wait_ge(ws, v)
            for o, i in pairs:
                e.dma_start(out=o, in_=i).then_inc(s, 16)

        self.ops[eng].append(run)
        self._commit((name, val), reads, writes)

    def custom(self, eng, fn, reads=(), writes=(), sem="cc", inc=1):
        waits = self._deps(eng, reads, writes)
        s = self._sem(sem)
        self.cnt[sem] += inc
        val = self.cnt[sem]

        def run(e, waits=waits, fn=fn, s=s, inc=inc):
            for ws, v in waits:
                e.wait_ge(ws, v)
            fn(e).then_inc(s, inc)

        self.ops[eng].append(run)
        self._commit((sem, val), reads, writes)

    def barrier(self, engines=None, skip=()):
        tot = {k: v for k, v in self.cnt.items() if k not in skip}
        for eng in engines or self.ENG:
            waits = []
            for n, v in tot.items():
                if v and self.waited[eng].get(n, 0) < v and not (n == eng):
                    self.waited[eng][n] = v
                    waits.append((self.sem[n], v))

            def run(e, waits=waits):
                for s, v in waits:
                    e.wait_ge(s, v)

            self.ops[eng].append(run)

    def emit(self):
        ops = self.ops
        with self.nc.Block() as block:

            @block.tensor
            def _(e):
                for f in ops["pe"]:
                    f(e)

            @block.scalar
            def _(e):
                for f in ops["act"]:
                    f(e)

            @block.vector
            def _(e):
                for f in ops["dve"]:
                    f(e)

            @block.gpsimd
            def _(e):
                for f in ops["pool"]:
                    f(e)

            @block.sync
            def _(e):
                for f in ops["sp"]:
                    f(e)

        self.ops = {e: [] for e in self.ENG}


def mm_group(P, out_ps, pairs, reads, writes):
    def fn(e):
        n = len(pairs)
        ins = None
        for i, (l, r) in enumerate(pairs):
            ins = e.matmul(out_ps, l, r, start=(i == 0), stop=(i == n - 1))
        return ins

    P.op("pe", fn, reads, writes)


class WStream:
    def __init__(self, P, wst, wbf, tag):
        self.P, self.wst, self.wbf, self.tag = P, wst, wbf, tag
        self.blocks = []
        self.loaded = 0
        self.done = -1

    def add(self, src_ap):
        self.blocks.append(src_ap)
        return len(self.blocks) - 1

    def ensure(self, upto):
        P = self.P
        upto = min(upto, len(self.blocks) - 1)
        assert upto - (self.done + 1) < len(self.wbf), (upto, self.done)
        while self.loaded <= upto:
            n = self.loaded
            si = n % len(self.wst)
            bi = n % len(self.wbf)
            st, bf = self.wst[si], self.wbf[bi]
            P.dma("sp", st[:, :], self.blocks[n], writes=[(self.tag, "st", si)], sem=f"{self.tag}st{si}")
            P.op("act", lambda e, st=st, bf=bf: e.activation(out=bf[:, :], in_=st[:, :], func=AF.Copy),
                 reads=[(self.tag, "st", si)], writes=[(self.tag, "bf", bi)])
            self.loaded += 1

    def slot(self, n):
        bi = n % len(self.wbf)
        return self.wbf[bi], (self.tag, "bf", bi)


def rstd_from_ssq(P, ps_ap, ps_key, out_ap, out_key, n):
    P.op("act", lambda e: e.activation(out=out_ap, in_=ps_ap, func=AF.Sqrt, bias=EPS, scale=1.0 / n),
         reads=[ps_key], writes=[out_key])
    P.op("dve", lambda e: e.reciprocal(out=out_ap, in_=out_ap), reads=[out_key], writes=[out_key])


def phase1(P, nc, T, NST=2, after_st=None):
    P.buf = {}
    TS = TOK // NST
    NTT = TS // 512
    with ExitStack() as es:
        def sb(name, shape, dt):
            return es.enter_context(nc.sbuf_tensor("s_" + name, shape, dt))

        xres = sb("xres", [128, KT, TS], F32)
        hT = sb("hT", [128, KT, TS], BF16)
        yT = sb("yT", [128, KT, TS], BF16)
        xh = sb("xh", [128, KT, 2], F32)
        hTh = sb("hTh", [128, KT, 2], BF16)
        wst = [sb(f"wst{i}", [128, KT * 128], F32) for i in range(2)]
        wbf = [sb(f"wbf{i}", [128, KT * 128], BF16) for i in range(8)]
        n0w = sb("n0w", [128, KT], F32)
        n1w = sb("n1w", [128, KT], F32)
        wcv = sb("wcv", [128, KT, 3], F32)
        ones = sb("ones", [128, 128], F32)
        acc = sb("acc", [128, TS], F32)
        sq = [sb(f"sq{i}", [128, 512], F32) for i in range(2)]
        rstd = sb("rstd", [128, TS], F32)
        acch = sb("acch", [128, 2], F32)
        sqh = sb("sqh", [128, 2], F32)
        rstdh = sb("rstdh", [128, 2], F32)
        hal = sb("hal", [128, KT, 2], F32)
        uh = sb("uh", [128, 2], F32)
        cu = [sb(f"cu{i}", [128, 514], F32) for i in range(2)]
        tsz = [sb(f"tsz{i}", [128, 512], F32) for i in range(2)]
        tu = [sb(f"tu{i}", [128, 512], F32) for i in range(2)]
        tacc = sb("tacc", [128, 512], F32)
        tbs = sb("tbs", [128, 512], F32)
        ps = [es.enter_context(nc.psum_tensor(f"ps{i}", [128, 512], F32)) for i in range(8)]

        P.dma("sp", n0w[:, :], T["n0w"][:, :], writes=["n0w"], sem="const", shared=True)
        P.dma("sp", n1w[:, :], T["n1w"][:, :], writes=["n1w"], sem="const", shared=True)
        P.dma("sp", wcv[:, :, :], T["wcv"][:, :, :], writes=["wcv"], sem="const", shared=True)
        P.dma("sp", xh[:, :, :], T["xh"][:, :, :], writes=["xh"], sem="const", shared=True)
        P.op("pool", lambda e: e.memset(ones[:, :], 1.0), writes=["ones"])

        W = WStream(P, wst, wbf, "w")
        widx = {}
        for st in range(NST):
            for e_ in range(KT):
                for g in range(4):
                    widx[("in", st, e_, g)] = W.add(T["w_in"][e_ * 4 + g, :, :])
            for j in range(KT):
                widx[("out", st, j)] = W.add(T["w_out"][j, :, :])
        LA = 3
        W.ensure(3)

        for kt in range(KT):
            P.op("act", lambda e, kt=kt: e.activation(out=sqh[:, :], in_=xh[:, kt, :], func=AF.Square),
                 reads=["xh"], writes=["sqh"])
            if kt == 0:
                P.op("pool", lambda e: e.tensor_copy(out=acch[:, :], in_=sqh[:, :]), reads=["sqh"], writes=["acch"])
            else:
                P.op("pool", lambda e: e.tensor_tensor(out=acch[:, :], in0=acch[:, :], in1=sqh[:, :], op=ALU.add),
                     reads=["sqh", "acch"], writes=["acch"])
        mm_group(P, ps[0][:, 0:2], [(ones[:, :], acch[:, :])], reads=["ones", "acch"], writes=[("ps", 0)])
        rstd_from_ssq(P, ps[0][:, 0:2], ("ps", 0), rstdh[:, :], "rstdh", D)
        for kt in range(KT):
            P.op("dve", lambda e, kt=kt: e.scalar_tensor_tensor(
                out=hTh[:, kt, :], in0=xh[:, kt, :], scalar=n0w[:, kt:kt + 1], in1=rstdh[:, :],
                op0=ALU.mult, op1=ALU.mult), reads=["xh", "n0w", "rstdh"], writes=["hTh"])

        bank = [1]

        def next_bank():
            b = bank[0]
            bank[0] = (b + 1) % 8
            return b

        for st in range(NST):
            t0 = st * TS
            P.dmas("sp", [(xres[:, kt, :], T["xT"][kt * 128:(kt + 1) * 128, t0:t0 + TS]) for kt in range(KT)],
                   writes=[("xres", kt, tt) for kt in range(KT) for tt in range(NTT)], sem="x", shared=True)
            for tt in range(NTT):
                sl = slice(tt * 512, (tt + 1) * 512)
                b = next_bank()
                for kt in range(KT):
                    s_ = sq[kt % 2]
                    P.op("act", lambda e, sl=sl, kt=kt, s_=s_: e.activation(out=s_[:, :], in_=xres[:, kt, sl], func=AF.Square),
                         reads=[("xres", kt, tt)], writes=[("sq", kt % 2)])
                    P.op("pe", lambda e, b=b, kt=kt, s_=s_: e.matmul(ps[b][:, :], ones[:, :], s_[:, :], start=(kt == 0), stop=(kt == KT - 1)),
                         reads=["ones", ("sq", kt % 2)], writes=[("ps", b)])
                rstd_from_ssq(P, ps[b][:, :], ("ps", b), rstd[:, sl], ("rstd", tt), D)
                for kt in range(KT):
                    eng = "dve"
                    P.op(eng, lambda e, sl=sl, kt=kt: e.scalar_tensor_tensor(
                        out=hT[:, kt, sl], in0=xres[:, kt, sl], scalar=n0w[:, kt:kt + 1], in1=rstd[:, sl],
                        op0=ALU.mult, op1=ALU.mult),
                        reads=[("xres", kt, tt), "n0w", ("rstd", tt)], writes=[("hT", kt, tt)])

            it = 0
            for e_ in range(KT):
                wi = [widx[("in", st, e_, g)] for g in range(4)]
                W.ensure(wi[3] + 4)
                if st == 0:
                    bh = next_bank()
                    for g, off in ((1, 0), (2, 2)):
                        wb, wk = W.slot(wi[g])
                        mm_group(P, ps[bh][:, off:off + 2],
                                 [(wb[:, kt * 128:(kt + 1) * 128], hTh[:, kt, :]) for kt in range(KT)],
                                 reads=[wk, "hTh"], writes=[("ps", bh)])
                    P.op("act", lambda e, bh=bh: e.activation(out=uh[:, :], in_=ps[bh][:, 2:4], func=AF.Copy),
                         reads=[("ps", bh)], writes=["uh"])
                    P.op("dve", lambda e, bh=bh, e_=e_: e.tensor_tensor(out=hal[:, e_, :], in0=ps[bh][:, 0:2], in1=uh[:, :], op=ALU.mult),
                         reads=[("ps", bh), "uh"], writes=[("hal", e_)])
                for tt in range(NTT):
                    sl = slice(tt * 512, (tt + 1) * 512)
                    bs_ = [next_bank() for _ in range(4)]
                    for g in range(4):
                        wb, wk = W.slot(wi[g])
                        mm_group(P, ps[bs_[g]][:, :],
                                 [(wb[:, kt * 128:(kt + 1) * 128], hT[:, kt, sl]) for kt in range(KT)],
                                 reads=[wk] + [("hT", kt, tt) for kt in range(KT)], writes=[("ps", bs_[g])])
                    pB, pC, pU, pZ = [ps[b] for b in bs_]
                    kB, kC, kU, kZ = [("ps", b) for b in bs_]
                    i2 = it % 2
                    it += 1
                    cu_, sz_, u_ = cu[i2], tsz[i2], tu[i2]
                    P.op("act", lambda e, pZ=pZ, sz_=sz_: e.activation(out=sz_[:, :], in_=pZ[:, :], func=AF.Silu),
                         reads=[kZ], writes=[("tsz", i2)])
                    P.op("act", lambda e, pU=pU, u_=u_: e.activation(out=u_[:, :], in_=pU[:, :], func=AF.Copy),
                         reads=[kU], writes=[("tu", i2)])
                    P.op("pool", lambda e, cu_=cu_, e_=e_: e.tensor_copy(out=cu_[:, 0:2], in_=hal[:, e_, :]),
                         reads=[("hal", e_)], writes=[("cuh", i2)])
                    P.op("dve", lambda e, pC=pC, cu_=cu_, u_=u_: e.tensor_tensor(out=cu_[:, 2:514], in0=pC[:, :], in1=u_[:, :], op=ALU.mult),
                         reads=[kC, ("tu", i2)], writes=[("cu", i2)])
                    P.op("pool", lambda e, cu_=cu_, e_=e_: e.tensor_copy(out=hal[:, e_, :], in_=cu_[:, 512:514]),
                         reads=[("cu", i2)], writes=[("hal", e_)])
                    P.op("dve", lambda e, cu_=cu_, e_=e_: e.tensor_scalar_mul(out=tacc[:, :], in0=cu_[:, 2:514], scalar1=wcv[:, e_, 2:3]),
                         reads=[("cu", i2), "wcv"], writes=["tacc"])
                    P.op("dve", lambda e, cu_=cu_, e_=e_: e.scalar_tensor_tensor(
                        out=tacc[:, :], in0=cu_[:, 1:513], scalar=wcv[:, e_, 1:2], in1=tacc[:, :], op0=ALU.mult, op1=ALU.add),
                        reads=[("cu", i2), ("cuh", i2), "wcv", "tacc"], writes=["tacc"])
                    P.op("dve", lambda e, cu_=cu_, e_=e_: e.scalar_tensor_tensor(
                        out=tacc[:, :], in0=cu_[:, 0:512], scalar=wcv[:, e_, 0:1], in1=tacc[:, :], op0=ALU.mult, op1=ALU.add),
                        reads=[("cu", i2), ("cuh", i2), "wcv", "tacc"], writes=["tacc"])
                    P.op("dve", lambda e, pB=pB, sz_=sz_: e.tensor_tensor(out=tbs[:, :], in0=pB[:, :], in1=sz_[:, :], op=ALU.mult),
                         reads=[kB, ("tsz", i2)], writes=["tbs"])
                    P.op("pool", lambda e, e_=e_, sl=sl: e.tensor_tensor(out=yT[:, e_, sl], in0=tacc[:, :], in1=tbs[:, :], op=ALU.mult),
                         reads=["tacc", "tbs"], writes=[("yT", e_, tt)])
                W.done = wi[3]

            for j in range(KT):
                wi = widx[("out", st, j)]
                W.ensure(wi + 7)
                wb, wk = W.slot(wi)
                for tt in range(NTT):
                    sl = slice(tt * 512, (tt + 1) * 512)
                    b = next_bank()
                    mm_group(P, ps[b][:, :],
                             [(wb[:, kt * 128:(kt + 1) * 128], yT[:, kt, sl]) for kt in range(KT)],
                             reads=[wk] + [("yT", kt, tt) for kt in range(KT)], writes=[("ps", b)])
                    P.op("dve", lambda e, b=b, j=j, sl=sl: e.tensor_tensor(out=xres[:, j, sl], in0=ps[b][:, :], in1=xres[:, j, sl], op=ALU.add),
                         reads=[("ps", b), ("xres", j, tt)], writes=[("xres", j, tt)])
                    if j == 0:
                        P.op("act", lambda e, sl=sl: e.activation(out=acc[:, sl], in_=xres[:, 0, sl], func=AF.Square),
                             reads=[("xres", 0, tt)], writes=[("acc", tt)])
                    else:
                        s_ = sq[j % 2]
                        P.op("act", lambda e, sl=sl, j=j, s_=s_: e.activation(out=s_[:, :], in_=xres[:, j, sl], func=AF.Square),
                             reads=[("xres", j, tt)], writes=[("sq", j % 2)])
                        P.op("pool", lambda e, sl=sl, s_=s_: e.tensor_tensor(out=acc[:, sl], in0=acc[:, sl], in1=s_[:, :], op=ALU.add),
                             reads=[("sq", j % 2), ("acc", tt)], writes=[("acc", tt)])
                W.done = wi
                P.dma("act", T["x1T"][j * 128:(j + 1) * 128, t0:t0 + TS], xres[:, j, :],
                      reads=[("xres", j, tt) for tt in range(NTT)], writes=[("x1T", j, st)], sem="x1st", shared=True)

            for tt in range(NTT):
                sl = slice(tt * 512, (tt + 1) * 512)
                b = next_bank()
                mm_group(P, ps[b][:, :], [(ones[:, :], acc[:, sl])], reads=["ones", ("acc", tt)], writes=[("ps", b)])
                rstd_from_ssq(P, ps[b][:, :], ("ps", b), rstd[:, sl], ("rstd", tt), D)
                for kt in range(KT):
                    eng = "dve"
                    P.op(eng, lambda e, sl=sl, kt=kt: e.scalar_tensor_tensor(
                        out=hT[:, kt, sl], in0=xres[:, kt, sl], scalar=n1w[:, kt:kt + 1], in1=rstd[:, sl],
                        op0=ALU.mult, op1=ALU.mult),
                        reads=[("xres", kt, tt), "n1w", ("rstd", tt)], writes=[("hT", kt, tt)])
            for kt in range(KT):
                if after_st is not None:
                    dst = T["h1T"][st * D + kt * 128:st * D + (kt + 1) * 128, :]
                else:
                    dst = T["h1T"][kt * 128:(kt + 1) * 128, t0:t0 + TS]
                P.dma("act", dst, hT[:, kt, :],
                      reads=[("hT", kt, tt) for tt in range(NTT)], writes=[("h1T", kt, st)], sem="h1st", shared=True)
            if after_st is not None:
                after_st(st)

        P.barrier(skip=("cc",) if after_st is not None else ())
        P.emit()


def transpose_op(P, out_ps, in_ap, ident, reads, writes):
    P.op("pe", lambda e: e.transpose(out_ps, in_ap, ident), reads, writes)


def phase2(P, nc, T, NTOK=SEQ, seg_out=False, after_seg=None):
    P.buf = {k: v for k, v in P.buf.items() if isinstance(k, tuple) and k[0] in ("h1all", "yall")}
    NG = NTOK // 512
    with ExitStack() as es:
        def sb(name, shape, dt):
            return es.enter_context(nc.sbuf_tensor("s2_" + name, shape, dt))

        wqk = sb("wqk", [128, KT, 512], BF16)
        wv = sb("wv", [128, KT, 512], BF16)
        wg = sb("wg", [128, KT, 512], BF16)
        wgk = sb("wgk", [128, KT, 16], BF16)
        stg = [sb(f"stg{i}", [128, 2048], F32) for i in range(2)]
        wgk2a = sb("wgk2a", [17, 256], F32)
        gnw = sb("gnw", [128, 512], F32)
        umask = sb("umask", [128, 128], F32)
        tri16 = sb("tri16", [128, 128], F32)
        ident = sb("ident", [128, 128], BF16)
        hg = [sb(f"hg{i}", [128, KT, 512], BF16) for i in range(2)]
        qk = [sb(f"qk{i}", [128, 4, 512], F32) for i in range(2)]
        gka = [sb(f"gka{i}", [32, 512], F32) for i in range(2)]
        S = sb("S", [128, 2, 512], F32)
        Sb = sb("Sb", [128, 2, 512], BF16)
        vb = [sb(f"vb{i}", [128, 512], BF16) for i in range(2)]
        sg = [sb(f"sg{i}", [128, 512], F32) for i in range(2)]
        m_ = sb("m", [128, 256], F32)
        na = sb("na", [128, 256], F32)
        ex = sb("ex", [128, 256], F32)
        ls = sb("ls", [128, 256], F32)
        eb = [sb(f"eb{i}", [128, 256], F32) for i in range(2)]
        enb = sb("enb", [128, 256], F32)
        qt = [sb(f"qt{i}", [128, 2, 128], BF16) for i in range(2)]
        kt_ = [sb(f"kt{i}", [128, 2, 128], BF16) for i in range(2)]
        At = sb("At", [128, 128], BF16)
        ktok = sb("ktok", [128, 256], BF16)
        junk = sb("junk", [128, 512], F32)
        ssq = sb("ssq", [128, 1], F32)
        rs = sb("rs", [128, 1], F32)
        epsb = sb("epsb", [128, 1], F32)
        oneb = sb("oneb", [128, 1], F32)
        y1 = sb("y1", [128, 512], F32)
        yb = sb("yb", [128, 512], BF16)
        yTs = [sb(f"yTs{i}", [128, 4, 512], BF16) for i in range(2)]
        ps = [es.enter_context(nc.psum_tensor(f"p2s{i}", [128, 512], F32)) for i in range(7)]
        pst = es.enter_context(nc.psum_tensor("p2t", [128, 1024], BF16))

        bank = [0]

        def nb():
            b = bank[0]
            bank[0] = (b + 1) % 7
            return b

        for name, dst, src in (("wgk2a", wgk2a, T["wgk2a"]), ("gnw", gnw, T["gnw"]), ("umask", umask, T["umask"]),
                               ("tri16", tri16, T["tri16"]), ("ident", ident, T["ident"])):
            P.dma("sp", dst[:, :], src[:, :], writes=[name], sem="c2", shared=True)
        for i in range(2):
            P.op("pool", lambda e, i=i: e.memset(gka[i][:, :], 1.0), writes=[("gka", i)])
        P.op("pool", lambda e: e.memset(epsb[:, :], EPS), writes=["epsb"])
        P.op("pool", lambda e: e.memset(oneb[:, :], 1.0), writes=["oneb"])
        P.op("pool", lambda e: e.memset(S[:, :, :], 0.0), writes=[("S", 0), ("S", 1)])
        P.op("pool", lambda e: e.memset(Sb[:, :, :], 0.0), writes=[("Sb", 0), ("Sb", 1)])
        n = 0
        for name, dst, src in (("wqk", wqk, T["wqk"]), ("wv", wv, T["wv"]), ("wg", wg, T["wg"])):
            for c in range(0, KT, 4):
                w_ = min(4, KT - c)
                si = n % 2
                n += 1
                P.dma("sp", stg[si][:, 0:w_ * 512], src[:, c * 512:(c + w_) * 512], writes=[("stg", si)], sem=f"stg{si}")
                if n % 2 == 0:
                    P.op("dve", lambda e, dst=dst, c=c, w_=w_, si=si: e.tensor_copy(
                        out=dst[:, c:c + w_, :], in_=stg[si][:, 0:w_ * 512].rearrange("p (k c) -> p k c", c=512)),
                        reads=[("stg", si)], writes=[(name, c)])
                else:
                    P.op("act", lambda e, dst=dst, c=c, w_=w_, si=si: e.activation(
                        out=dst[:, c:c + w_, :], in_=stg[si][:, 0:w_ * 512].rearrange("p (k c) -> p k c", c=512), func=AF.Copy),
                        reads=[("stg", si)], writes=[(name, c)])
        si = n % 2
        P.dma("sp", stg[si][:, 0:KT * 16], T["wgk"][:, :], writes=[("stg", si)], sem=f"stg{si}")
        P.op("pool", lambda e, si=si: e.tensor_copy(out=wgk[:, :, :], in_=stg[si][:, 0:KT * 16].rearrange("p (k c) -> p k c", c=16)),
             reads=[("stg", si)], writes=["wgk"])
        wkeys = [("wqk", c) for c in range(0, KT, 4)]
        vkeys = [("wv", c) for c in range(0, KT, 4)]
        gkeys = [("wg", c) for c in range(0, KT, 4)]

        NCH = NG * 4
        At2 = [At, sb("At1", [128, 128], BF16)]
        ktok2 = [ktok, sb("ktok1", [128, 256], BF16)]

        def seg_off(g):
            return divmod(g * 512, TOK)

        def load_group(g):
            seg, off = seg_off(g)
            hg_ = hg[g % 2]
            if seg_out:
                st_, col = divmod(off, 1024)
                srcs = [T["h1"][(st_ * 8 + kt // 2) * 1024 + seg * 256 + (kt % 2) * 128:(st_ * 8 + kt // 2) * 1024 + seg * 256 + (kt % 2) * 128 + 128,
                                col:col + 512] for kt in range(KT)]
            else:
                srcs = [T["h1"][seg * D + kt * 128: seg * D + (kt + 1) * 128, off:off + 512] for kt in range(KT)]
            rd = [("h1all", st_, kp) for kp in range(8)] if seg_out else []
            P.dmas("sp", [(hg_[:, kt, :], srcs[kt]) for kt in range(KT)], reads=rd, writes=[("hg", g % 2)], sem=f"hg{g % 2}")

        def proj_group(g):
            hg_, hk, qk_ = hg[g % 2], ("hg", g % 2), qk[g % 2]
            for dt in range(4):
                b = nb()
                mm_group(P, ps[b][:, :], [(wqk[:, kt, dt * 128:(dt + 1) * 128], hg_[:, kt, :]) for kt in range(KT)],
                         reads=wkeys + [hk], writes=[("ps", b)])
                P.op("act", lambda e, b=b, dt=dt, qk_=qk_: e.activation(out=qk_[:, dt, :], in_=ps[b][:, :], func=AF.Copy),
                     reads=[("ps", b)], writes=[("qk", g % 2, dt)])
            b = nb()
            mm_group(P, ps[b][0:16, :], [(wgk[:, kt, :], hg_[:, kt, :]) for kt in range(KT)], reads=["wgk", hk], writes=[("ps", b)])
            gka_ = gka[g % 2]
            P.op("act", lambda e, b=b, gka_=gka_: e.activation(out=gka_[0:16, :], in_=ps[b][0:16, :], func=AF.Copy),
                 reads=[("ps", b)], writes=[("gka", g % 2)])

        def ctx(n):
            g, c = divmod(n, 4)
            return g, c, n % 2, slice(c * 128, (c + 1) * 128), hg[g % 2], ("hg", g % 2), qk[g % 2], gka[g % 2]

        A = {}

        def stageA1(n):
            g, c, i2, cs, hg_, hk, qk_, gka_ = ctx(n)
            bl = nb()
            mm_group(P, ps[bl][:, 0:256], [(gka_[0:17, cs], wgk2a[0:17, :])], reads=[("gka", g % 2), "wgk2a"], writes=[("ps", bl)])
            bv = nb()
            mm_group(P, ps[bv][:, :], [(hg_[:, kt, cs], wv[:, kt, :]) for kt in range(KT)], reads=vkeys + [hk], writes=[("ps", bv)])
            P.op("dve", lambda e, bl=bl: e.tensor_scalar_min(out=m_[:, :], in0=ps[bl][:, 0:256], scalar1=0.0),
                 reads=[("ps", bl)], writes=["m"])
            P.op("dve", lambda e, bl=bl: e.scalar_tensor_tensor(out=na[:, :], in0=m_[:, :], scalar=2.0, in1=ps[bl][:, 0:256],
                                                                 op0=ALU.mult, op1=ALU.subtract),
                 reads=[("ps", bl), "m"], writes=["na"])
            P.op("act", lambda e: e.activation(out=ex[:, :], in_=na[:, :], func=AF.Exp), reads=["na"], writes=["ex"])
            P.op("act", lambda e: e.activation(out=ex[:, :], in_=ex[:, :], func=AF.Ln, bias=1.0), reads=["ex"], writes=["ex"])
            P.op("dve", lambda e: e.tensor_tensor(out=ls[:, :], in0=m_[:, :], in1=ex[:, :], op=ALU.subtract),
                 reads=["m", "ex"], writes=["ls"])
            P.op("act", lambda e, bv=bv, i2=i2: e.activation(out=vb[i2][:, :], in_=ps[bv][:, :], func=AF.Copy),
                 reads=[("ps", bv)], writes=[("vb", i2)])

        def stageA2(n):
            g, c, i2, cs, hg_, hk, qk_, gka_ = ctx(n)
            bg = nb()
            mm_group(P, ps[bg][:, :], [(hg_[:, kt, cs], wg[:, kt, :]) for kt in range(KT)], reads=gkeys + [hk], writes=[("ps", bg)])
            P.op("act", lambda e, bg=bg, i2=i2: e.activation(out=sg[i2][:, :], in_=ps[bg][:, :], func=AF.Exp, scale=-1.0),
                 reads=[("ps", bg)], writes=[("sg", i2)])
            P.op("act", lambda e, i2=i2: e.activation(out=sg[i2][:, :], in_=sg[i2][:, :], func=AF.Copy, bias=1.0),
                 reads=[("sg", i2)], writes=[("sg", i2)])
            P.op("dve", lambda e, i2=i2: e.reciprocal(out=sg[i2][:, :], in_=sg[i2][:, :]), reads=[("sg", i2)], writes=[("sg", i2)])
            P.op("dve", lambda e, bg=bg, i2=i2: e.tensor_tensor(out=sg[i2][:, :], in0=ps[bg][:, :], in1=sg[i2][:, :], op=ALU.mult),
                 reads=[("ps", bg), ("sg", i2)], writes=[("sg", i2)])

        def stageA3(n):
            g, c, i2, cs, hg_, hk, qk_, gka_ = ctx(n)
            bb = nb()
            for dt in range(2):
                mm_group(P, ps[bb][:, dt * 128:(dt + 1) * 128], [(ls[:, dt * 128:(dt + 1) * 128], tri16[:, :])],
                         reads=["ls", "tri16"], writes=[("ps", bb)])
            eb_ = eb[i2]
            P.op("act", lambda e, bb=bb, eb_=eb_: e.activation(out=eb_[:, :], in_=ps[bb][:, 0:256], func=AF.Exp),
                 reads=[("ps", bb)], writes=[("eb", i2)])
            P.op("act", lambda e, bb=bb: e.activation(out=enb[:, :], in_=ps[bb][:, 0:256], func=AF.Exp, scale=-1.0),
                 reads=[("ps", bb)], writes=["enb"])
            qt_, kk_ = qt[i2], kt_[i2]
            for dt in range(2):
                P.op("dve", lambda e, dt=dt, qt_=qt_, eb_=eb_, qk_=qk_, cs=cs: e.scalar_tensor_tensor(
                    out=qt_[:, dt, :], in0=qk_[:, dt, cs], scalar=HK ** -0.5, in1=eb_[:, dt * 128:(dt + 1) * 128],
                    op0=ALU.mult, op1=ALU.mult), reads=[("qk", g % 2, dt), ("eb", i2)], writes=[("qt", i2, dt)])
                P.op("dve", lambda e, dt=dt, kk_=kk_, qk_=qk_, cs=cs: e.tensor_tensor(
                    out=kk_[:, dt, :], in0=qk_[:, 2 + dt, cs], in1=enb[:, dt * 128:(dt + 1) * 128], op=ALU.mult),
                    reads=[("qk", g % 2, 2 + dt), "enb"], writes=[("kt", i2, dt)])

        def stageA4(n):
            g, c, i2, cs, hg_, hk, qk_, gka_ = ctx(n)
            qt_, kk_ = qt[i2], kt_[i2]
            ba = nb()
            mm_group(P, ps[ba][:, 0:128], [(kk_[:, dt, :], qt_[:, dt, :]) for dt in range(2)],
                     reads=[("kt", i2, 0), ("kt", i2, 1), ("qt", i2, 0), ("qt", i2, 1)], writes=[("ps", ba)])
            for dt in range(2):
                transpose_op(P, pst[:, dt * 128:(dt + 1) * 128], kk_[:, dt, :], ident[:, :],
                             reads=[("kt", i2, dt), "ident"], writes=["pst"])
            P.op("dve", lambda e, ba=ba, i2=i2: e.tensor_tensor(out=At2[i2][:, :], in0=ps[ba][:, 0:128], in1=umask[:, :], op=ALU.mult),
                 reads=[("ps", ba), "umask"], writes=[("At", i2)])
            P.op("act", lambda e, i2=i2: e.activation(out=ktok2[i2][:, :], in_=pst[:, 0:256], func=AF.Copy),
                 reads=["pst"], writes=[("ktok", i2)])

        def stageA(n):
            stageA1(n)
            stageA2(n)
            stageA3(n)
            stageA4(n)

        def stageB1(n):
            i2 = n % 2
            qt_, eb_ = qt[i2], eb[i2]
            bo = nb()
            A[n] = bo
            mm_group(P, ps[bo][:, :], [(At2[i2][:, :], vb[i2][:, :]), (qt_[:, 0, :], Sb[:, 0, :]), (qt_[:, 1, :], Sb[:, 1, :])],
                     reads=[("At", i2), ("vb", i2), ("qt", i2, 0), ("qt", i2, 1), ("Sb", 0), ("Sb", 1)], writes=[("ps", bo)])
            for dt in range(2):
                bs_ = nb()
                mm_group(P, ps[bs_][:, :], [(ktok2[i2][:, dt * 128:(dt + 1) * 128], vb[i2][:, :])],
                         reads=[("ktok", i2), ("vb", i2)], writes=[("ps", bs_)])
                P.op("dve", lambda e, bs_=bs_, dt=dt: e.tensor_tensor(out=S[:, dt, :], in0=ps[bs_][:, :], in1=S[:, dt, :], op=ALU.add),
                     reads=[("ps", bs_), ("S", dt)], writes=[("S", dt)])
                P.op("act", lambda e, dt=dt, eb_=eb_: e.activation(out=S[:, dt, :], in_=S[:, dt, :], func=AF.Copy,
                                                                    scale=eb_[:, dt * 128 + 127:dt * 128 + 128]),
                     reads=[("S", dt), ("eb", i2)], writes=[("S", dt)])
                P.op("pool", lambda e, dt=dt: e.tensor_copy(out=Sb[:, dt, :], in_=S[:, dt, :]), reads=[("S", dt)], writes=[("Sb", dt)])

        def stageB2(n):
            i2 = n % 2
            bo = A.pop(n)
            P.op("pool", lambda e: e.memset(ssq[:, :], 0.0), writes=["ssq"])
            P.op("act", lambda e, bo=bo: e.activation(out=junk[:, :], in_=ps[bo][:, :], func=AF.Square, accum_out=ssq[:, :]),
                 reads=[("ps", bo)], writes=["junk", "ssq"])
            P.op("act", lambda e: e.activation(out=rs[:, :], in_=ssq[:, :], func=AF.Ln, bias=epsb[:, :], scale=1.0 / HV),
                 reads=["ssq", "epsb"], writes=["rs"])
            P.op("act", lambda e: e.activation(out=rs[:, :], in_=rs[:, :], func=AF.Exp, scale=-0.5), reads=["rs"], writes=["rs"])
            P.op("dve", lambda e, bo=bo: e.scalar_tensor_tensor(out=y1[:, :], in0=ps[bo][:, :], scalar=rs[:, 0:1], in1=gnw[:, :],
                                                                 op0=ALU.mult, op1=ALU.mult),
                 reads=[("ps", bo), "rs", "gnw"], writes=["y1"])
            P.op("dve", lambda e, i2=i2: e.tensor_tensor(out=yb[:, :], in0=y1[:, :], in1=sg[i2][:, :], op=ALU.mult),
                 reads=["y1", ("sg", i2)], writes=["yb"])

        def stageC(n):
            g, c = divmod(n, 4)
            cs = slice(c * 128, (c + 1) * 128)
            yT_ = yTs[g % 2]
            for et in range(4):
                transpose_op(P, pst[:, 512 + et * 128:512 + (et + 1) * 128], yb[:, et * 128:(et + 1) * 128], ident[:, :],
                             reads=["yb", "ident"], writes=["pst"])
            P.op("act", lambda e, yT_=yT_, cs=cs: e.activation(
                out=yT_[:, :, cs], in_=pst[:, 512:1024].rearrange("p (e t) -> p e t", t=128), func=AF.Copy),
                reads=["pst"], writes=[("yTs", g % 2, c)])
            if c == 3:
                seg, off = seg_off(g)
                if seg_out:
                    half, col = divmod(off, 1024)
                    dsts = [T["yseg"][((seg * 2 + half) * 4 + et) * 128:((seg * 2 + half) * 4 + et + 1) * 128, col:col + 512] for et in range(4)]
                else:
                    dsts = [T["yT"][et * 128:(et + 1) * 128, g * 512:(g + 1) * 512] for et in range(4)]
                P.dmas("act", [(dsts[et], yT_[:, et, :]) for et in range(4)],
                       reads=[("yTs", g % 2, cc_) for cc_ in range(4)], writes=[("yTd", g)], sem=f"yst{g % 2}")
                if after_seg is not None and g % 2 == 1:
                    after_seg(g // 2)

        load_group(0)
        proj_group(0)
        if NG > 1:
            load_group(1)
        stageA(0)
        for n in range(NCH):
            nxt = n + 1 < NCH
            if nxt:
                g1, c1 = divmod(n + 1, 4)
                if c1 == 0:
                    proj_group(g1)
                    if g1 + 1 < NG:
                        load_group(g1 + 1)
                stageA1(n + 1)
                stageA2(n + 1)
                stageA3(n + 1)
            if n >= 1:
                stageC(n - 1)
            stageB1(n)
            if nxt:
                stageA4(n + 1)
            stageB2(n)
        stageC(NCH - 1)
        P.barrier(skip=("cc",) if after_seg is not None else ())
        P.emit()


def phase3(P, nc, T, indirect=False):
    P.buf = {k: v for k, v in P.buf.items() if isinstance(k, tuple) and k[0] in ("h1all", "yall")}
    NTT = TOK // 512
    with ExitStack() as es:
        def sb(name, shape, dt):
            return es.enter_context(nc.sbuf_tensor("s3_" + name, shape, dt))

        yTs = sb("yT", [128, KT, TOK], BF16)
        wo = [sb(f"wo{j}", [128, KT * 128], BF16) for j in range(KT)]
        stg = [sb(f"stg{i}", [128, KT * 128], F32) for i in range(2)]
        x2 = sb("x2", [128, KT, 512], F32)
        nfw = sb("nfw", [128, KT], F32)
        ones = sb("ones", [128, 128], F32)
        acc = sb("acc", [128, 512], F32)
        sq = [sb(f"sq{i}", [128, 512], F32) for i in range(2)]
        rstd = sb("rstd", [128, 512], F32)
        ps = [es.enter_context(nc.psum_tensor(f"p3s{i}", [128, 512], F32)) for i in range(8)]
        P.dma("sp", nfw[:, :], T["nfw"][:, :], writes=["nfw"], sem="c3", shared=True)
        P.op("pool", lambda e: e.memset(ones[:, :], 1.0), writes=["ones"])
        if indirect:
            ix = es.enter_context(nc.sbuf_tensor("s3_ix", [128, 2 * KT], mybir.dt.int32))
            P.dma("sp", ix[:, :], T["yidx"][:, :], writes=["ix"], sem="c3", shared=True)
            for half in range(2):
                for kt in range(KT):
                    q = half * KT + kt
                    P.custom("pool", lambda e, kt=kt, half=half, q=q: e.indirect_dma_start(
                        out=yTs[:, kt, half * 1024:(half + 1) * 1024], out_offset=None, in_=T["yall"][:, :],
                        in_offset=bass.IndirectOffsetOnAxis(ap=ix[:, q:q + 1], axis=0)),
                        reads=["ix"] + [("yall", c) for c in range(32)], writes=["yT"] if q == 2 * KT - 1 else [("yTpart", q)],
                        sem="d_yind", inc=16)
            P.shared.add("d_yind")
        for j in range(KT):
            si = j % 2
            P.dma("sp", stg[si][:, :], T["w_out"][j, :, :], writes=[("stg", si)], sem=f"stg{si}")
            if j % 2 == 0:
                P.op("dve", lambda e, j=j, si=si: e.tensor_copy(out=wo[j][:, :], in_=stg[si][:, :]), reads=[("stg", si)], writes=[("wo", j)])
            else:
                P.op("act", lambda e, j=j, si=si: e.activation(out=wo[j][:, :], in_=stg[si][:, :], func=AF.Copy),
                     reads=[("stg", si)], writes=[("wo", j)])
            if j == 1 and not indirect:
                P.dmas("sp", [(yTs[:, kt, :], T["yT"][kt * 128:(kt + 1) * 128, :]) for kt in range(KT)],
                       writes=["yT"], sem="y3", shared=True)
        bank = [0]
        for tt in range(NTT):
            sl = slice(tt * 512, (tt + 1) * 512)
            P.dmas("sp", [(x2[:, j, :], T["x1T"][j * 128:(j + 1) * 128, sl]) for j in range(KT)],
                   writes=[("x2", j) for j in range(KT)], sem="x3", shared=True)
            for j in range(KT):
                b = bank[0]
                bank[0] = (b + 1) % 8
                mm_group(P, ps[b][:, :], [(wo[j][:, kt * 128:(kt + 1) * 128], yTs[:, kt, sl]) for kt in range(KT)],
                         reads=[("wo", j), "yT"], writes=[("ps", b)])
                P.op("dve", lambda e, b=b, j=j: e.tensor_tensor(out=x2[:, j, :], in0=ps[b][:, :], in1=x2[:, j, :], op=ALU.add),
                     reads=[("ps", b), ("x2", j)], writes=[("x2", j)])
                if j == 0:
                    P.op("act", lambda e: e.activation(out=acc[:, :], in_=x2[:, 0, :], func=AF.Square), reads=[("x2", 0)], writes=["acc"])
                else:
                    s_ = sq[j % 2]
                    P.op("act", lambda e, j=j, s_=s_: e.activation(out=s_[:, :], in_=x2[:, j, :], func=AF.Square),
                         reads=[("x2", j)], writes=[("sq", j % 2)])
                    P.op("pool", lambda e, s_=s_: e.tensor_tensor(out=acc[:, :], in0=acc[:, :], in1=s_[:, :], op=ALU.add),
                         reads=[("sq", j % 2), "acc"], writes=["acc"])
            b = bank[0]
            bank[0] = (b + 1) % 8
            mm_group(P, ps[b][:, :], [(ones[:, :], acc[:, :])], reads=["ones", "acc"], writes=[("ps", b)])
            rstd_from_ssq(P, ps[b][:, :], ("ps", b), rstd[:, :], "rstd", D)
            for j in range(KT):
                P.op("dve", lambda e, j=j: e.scalar_tensor_tensor(out=x2[:, j, :], in0=x2[:, j, :], scalar=nfw[:, j:j + 1], in1=rstd[:, :],
                                                                   op0=ALU.mult, op1=ALU.mult),
                     reads=[("x2", j), "nfw", "rstd"], writes=[("x2", j)])
            P.dmas("act", [(T["oT"][j * 128:(j + 1) * 128, sl], x2[:, j, :]) for j in range(KT)],
                   reads=[("x2", j) for j in range(KT)], writes=[("oT", tt)], sem="o3", shared=True)
        P.barrier()
        P.emit()


def new_nc():
    nc = bass.Bass("TRN2", target_bir_lowering=False)
    return nc


def build_p1():
    nc = new_nc()
    T = {}
    T["xT"] = nc.dram_tensor("xT", [D, TOK], F32, kind="ExternalInput")
    T["xh"] = nc.dram_tensor("xh", [128, KT, 2], F32, kind="ExternalInput")
    T["n0w"] = nc.dram_tensor("n0w", [128, KT], F32, kind="ExternalInput")
    T["n1w"] = nc.dram_tensor("n1w", [128, KT], F32, kind="ExternalInput")
    T["wcv"] = nc.dram_tensor("wcv", [128, KT, 3], F32, kind="ExternalInput")
    T["w_in"] = nc.dram_tensor("w_in", [4 * KT, 128, KT * 128], F32, kind="ExternalInput")
    T["w_out"] = nc.dram_tensor("w_out", [KT, 128, KT * 128], F32, kind="ExternalInput")
    T["x1T"] = nc.dram_tensor("x1T", [D, TOK], F32, kind="ExternalOutput")
    T["h1T"] = nc.dram_tensor("h1T", [D, TOK], BF16, kind="ExternalOutput")
    with ExitStack() as es:
        P = Prog(nc, es)
        phase1(P, nc, T)
    return nc


def vec_pk(w):
    return np.ascontiguousarray(w.reshape(KT, 128).T)


def wblocks(w, ncols_blk=128):
    K, N = w.shape
    nb = N // 128
    a = w.reshape(KT, 128, nb, 128)
    a = a.transpose(2, 1, 0, 3)
    return np.ascontiguousarray(a.reshape(nb, 128, KT * 128))


def p1_inputs(x, norm0_w, conv0_w_in, conv0_w_conv, conv0_w_out, norm1_w):
    E = D
    wb = wblocks(conv0_w_in)
    wb = wb.reshape(4, 16, 128, 2048).transpose(1, 0, 2, 3).reshape(64, 128, 2048)
    wb = np.ascontiguousarray(wb)
    wo = wblocks(conv0_w_out)
    wcv = np.ascontiguousarray(conv0_w_conv.reshape(KT, 128, 3).transpose(1, 0, 2))
    n0 = vec_pk(norm0_w)
    n1 = vec_pk(norm1_w)
    maps = []
    for c in range(NCORES):
        b, s = divmod(c, 4)
        xs = x[b, s * TOK:(s + 1) * TOK, :]
        xT = np.ascontiguousarray(xs.T)
        if s == 0:
            hal = np.zeros((2, D), np.float32)
        else:
            hal = x[b, s * TOK - 2:s * TOK, :]
        xh = np.ascontiguousarray(hal.T.reshape(KT, 128, 2).transpose(1, 0, 2))
        maps.append({"xT": xT, "xh": xh, "n0w": n0, "n1w": n1, "wcv": wcv, "w_in": wb, "w_out": wo})
    return maps


def build_p2(ntok=SEQ):
    nc = new_nc()
    T = {}
    T["h1"] = nc.dram_tensor("h1", [(ntok // TOK) * D, TOK], BF16, kind="ExternalInput")
    T["wqk"] = nc.dram_tensor("wqk", [128, KT * 512], F32, kind="ExternalInput")
    T["wv"] = nc.dram_tensor("wv", [128, KT * 512], F32, kind="ExternalInput")
    T["wg"] = nc.dram_tensor("wg", [128, KT * 512], F32, kind="ExternalInput")
    T["wgk"] = nc.dram_tensor("wgk", [128, KT * 16], F32, kind="ExternalInput")
    T["wgk2a"] = nc.dram_tensor("wgk2a", [17, 256], F32, kind="ExternalInput")
    T["gnw"] = nc.dram_tensor("gnw", [128, 512], F32, kind="ExternalInput")
    T["umask"] = nc.dram_tensor("umask", [128, 128], F32, kind="ExternalInput")
    T["tri16"] = nc.dram_tensor("tri16", [128, 128], F32, kind="ExternalInput")
    T["ident"] = nc.dram_tensor("ident", [128, 128], BF16, kind="ExternalInput")
    T["yT"] = nc.dram_tensor("yT", [HV, ntok], BF16, kind="ExternalOutput")
    with ExitStack() as es:
        P = Prog(nc, es)
        phase2(P, nc, T, ntok)
    return nc


def build_p3():
    nc = new_nc()
    T = {}
    T["yT"] = nc.dram_tensor("yT", [D, TOK], BF16, kind="ExternalInput")
    T["x1T"] = nc.dram_tensor("x1T", [D, TOK], F32, kind="ExternalInput")
    T["w_out"] = nc.dram_tensor("w_out", [KT, 128, KT * 128], F32, kind="ExternalInput")
    T["nfw"] = nc.dram_tensor("nfw", [128, KT], F32, kind="ExternalInput")
    T["oT"] = nc.dram_tensor("oT", [D, TOK], F32, kind="ExternalOutput")
    with ExitStack() as es:
        P = Prog(nc, es)
        phase3(P, nc, T)
    return nc


def pk_cols(w):
    C = w.shape[1]
    return np.ascontiguousarray(w.reshape(KT, 128, C).transpose(1, 0, 2).reshape(128, KT * C))


def p2_consts():
    jj, ii = np.meshgrid(np.arange(128), np.arange(128), indexing="ij")
    um = (jj <= ii).astype(np.float32)
    return {"umask": um, "tri16": (um / 16.0).astype(np.float32), "ident": np.eye(128, dtype=np.float32).astype(ml_dtypes.bfloat16)}


def p2_weights(h, gla1_w_in, gla1_w_gk2, gla1_b_gk2, gla1_gn_w):
    KD, VD = 4 * HK, 4 * HV
    wq = gla1_w_in[:, h * HK:(h + 1) * HK]
    wk = gla1_w_in[:, KD + h * HK:KD + (h + 1) * HK]
    wv = gla1_w_in[:, 2 * KD + h * HV:2 * KD + (h + 1) * HV]
    wg = gla1_w_in[:, 2 * KD + VD + h * HV:2 * KD + VD + (h + 1) * HV]
    wgk = gla1_w_in[:, 2 * KD + 2 * VD:]
    m = {"wqk": pk_cols(np.concatenate([wq, wk], axis=1)), "wv": pk_cols(wv), "wg": pk_cols(wg), "wgk": pk_cols(wgk),
         "wgk2a": np.ascontiguousarray(np.concatenate([gla1_w_gk2[:, h * HK:(h + 1) * HK], gla1_b_gk2[None, h * HK:(h + 1) * HK]], axis=0)),
         "gnw": np.ascontiguousarray(np.broadcast_to(gla1_gn_w[None, :], (128, HV)))}
    m.update(p2_consts())
    return m


def build_fused():
    nc = new_nc()
    EI = dict(kind="ExternalInput")
    T1 = {"xT": nc.dram_tensor("xT", [D, TOK], F32, **EI), "xh": nc.dram_tensor("xh", [128, KT, 2], F32, **EI),
          "n0w": nc.dram_tensor("n0w", [128, KT], F32, **EI), "n1w": nc.dram_tensor("n1w", [128, KT], F32, **EI),
          "wcv": nc.dram_tensor("wcv", [128, KT, 3], F32, **EI),
          "w_in": nc.dram_tensor("w_in", [4 * KT, 128, KT * 128], F32, **EI),
          "w_out": nc.dram_tensor("w_out0", [KT, 128, KT * 128], F32, **EI),
          "x1T": nc.dram_tensor("x1T_d", [D, TOK], F32), "h1T": nc.dram_tensor("h1T_d", [2 * D, TOK // 2], BF16)}
    h1all = nc.dram_tensor("h1all_d", [16 * 1024, TOK // 2], BF16)
    T2 = {"h1": h1all, "wqk": nc.dram_tensor("wqk", [128, KT * 512], F32, **EI), "wv": nc.dram_tensor("wv", [128, KT * 512], F32, **EI),
          "wg": nc.dram_tensor("wg", [128, KT * 512], F32, **EI), "wgk": nc.dram_tensor("wgk", [128, KT * 16], F32, **EI),
          "wgk2a": nc.dram_tensor("wgk2a", [17, 256], F32, **EI), "gnw": nc.dram_tensor("gnw", [128, 512], F32, **EI),
          "umask": nc.dram_tensor("umask", [128, 128], F32, **EI), "tri16": nc.dram_tensor("tri16", [128, 128], F32, **EI),
          "ident": nc.dram_tensor("ident", [128, 128], BF16, **EI), "yseg": nc.dram_tensor("yseg_d", [32 * 128, TOK // 2], BF16)}
    yall = nc.dram_tensor("yall_d", [32 * 512, TOK // 2], BF16)
    T3 = {"yall": yall, "yidx": nc.dram_tensor("yidx", [128, 2 * KT], mybir.dt.int32, **EI), "x1T": T1["x1T"],
          "w_out": nc.dram_tensor("w_out1", [KT, 128, KT * 128], F32, **EI), "nfw": nc.dram_tensor("nfw", [128, KT], F32, **EI),
          "oT": nc.dram_tensor("oT", [D, TOK], F32, kind="ExternalOutput")}
    groups = [[0, 1, 2, 3], [4, 5, 6, 7]]
    with ExitStack() as es:
        P = Prog(nc, es)
        def ag1(st):
            for kp in range(8):
                P.custom("pool", lambda e, st=st, kp=kp: e.collective_compute(
                    "AllGather", ALU.bypass, replica_groups=groups,
                    ins=[T1["h1T"][st * D + kp * 256:st * D + (kp + 1) * 256, :].opt()],
                    outs=[h1all[(st * 8 + kp) * 1024:(st * 8 + kp + 1) * 1024, :].opt()]),
                    reads=[("h1T", 2 * kp, st), ("h1T", 2 * kp + 1, st)], writes=[("h1all", st, kp)], sem="cc", inc=1)

        def ag2(hidx):
            for et in range(4):
                c = hidx * 4 + et
                P.custom("pool", lambda e, c=c: e.collective_compute(
                    "AllGather", ALU.bypass, replica_groups=groups,
                    ins=[T2["yseg"][c * 128:(c + 1) * 128, :].opt()], outs=[yall[c * 512:(c + 1) * 512, :].opt()]),
                    reads=[("yTd", 2 * hidx), ("yTd", 2 * hidx + 1)], writes=[("yall", c)], sem="cc", inc=1)

        phase1(P, nc, T1, after_st=ag1)
        phase2(P, nc, T2, SEQ, seg_out=True, after_seg=ag2)
        phase3(P, nc, T3, indirect=True)
    return nc


def kernel(x, norm0_w, conv0_w_in, conv0_w_conv, conv0_w_out, norm1_w, gla1_w_in, gla1_w_gk2,
           gla1_b_gk2, gla1_gn_w, gla1_w_out, norm_f_w):
    f = lambda a: np.asarray(a, dtype=np.float32)
    x, norm0_w, conv0_w_in, conv0_w_conv, conv0_w_out, norm1_w = map(f, (x, norm0_w, conv0_w_in, conv0_w_conv, conv0_w_out, norm1_w))
    gla1_w_in, gla1_w_gk2, gla1_b_gk2, gla1_gn_w, gla1_w_out, norm_f_w = map(f, (gla1_w_in, gla1_w_gk2, gla1_b_gk2, gla1_gn_w, gla1_w_out, norm_f_w))
    cores = list(range(NCORES))
    maps = p1_inputs(x, norm0_w, conv0_w_in, conv0_w_conv, conv0_w_out, norm1_w)
    wo1 = wblocks(gla1_w_out)
    nf = vec_pk(norm_f_w)
    pp, kk = np.meshgrid(np.arange(128), np.arange(KT), indexing="ij")
    for c in cores:
        b, hs = divmod(c, 4)
        m = maps[c]
        m["w_out0"] = m.pop("w_out")
        m.update(p2_weights(hs, gla1_w_in, gla1_w_gk2, gla1_b_gk2, gla1_gn_w))
        m["w_out1"] = wo1
        m["nfw"] = nf
        m["yidx"] = np.ascontiguousarray(np.concatenate(
            [(((hs * 2 + half) * 4 + kk % 4) * 512 + (kk // 4) * 128 + pp) for half in range(2)], axis=1).astype(np.int32))
    res = run_bass_kernel_spmd(build_fused(), maps, core_ids=cores).results
    out = np.empty((NB, SEQ, D), np.float32)
    for c in cores:
        b, s = divmod(c, 4)
        out[b, s * TOK:(s + 1) * TOK, :] = np.asarray(res[c]["oT"]).T
    return out
```

```python
from contextlib import ExitStack

import numpy as np
import ml_dtypes

import concourse.bass as bass
import concourse.mybir as mybir
from concourse.bass_utils import run_bass_kernel_spmd

F32 = mybir.dt.float32
BF16 = mybir.dt.bfloat16
AF = mybir.ActivationFunctionType
ALU = mybir.AluOpType

D = 2048
SEQ = 8192
NB = 2
NCORES = 8
TOK = 2048
EPS = 1e-6
KT = 16
HK = 256
HV = 512


class Prog:
    ENG = ("pe", "act", "dve", "pool", "sp")

    def __init__(self, nc, es):
        self.nc, self.es = nc, es
        self.ops = {e: [] for e in self.ENG}
        self.sem = {}
        self.cnt = {}
        self.shared = set()
        self.waited = {e: {} for e in self.ENG}
        self.buf = {}

    def _sem(self, name):
        if name not in self.sem:
            self.sem[name] = self.es.enter_context(self.nc.semaphore(name))
            self.cnt[name] = 0
        return self.sem[name]

    def _deps(self, eng, reads, writes):
        ev = {}

        def add(n, v):
            if n in self.shared:
                v = self.cnt[n]
            if ev.get(n, 0) < v:
                ev[n] = v

        for k in reads:
            b = self.buf.get(k)
            if b and b["w"]:
                add(*b["w"])
        for k in writes:
            b = self.buf.get(k)
            if b:
                if b["w"]:
                    add(*b["w"])
                for n, v in b["r"].items():
                    add(n, v)
        out = []
        for n, v in ev.items():
            if eng == "pe" and n == "pe":
                continue
            if self.waited[eng].get(n, 0) < v:
                self.waited[eng][n] = v
                out.append((self.sem[n], v))
        return out

    def _commit(self, event, reads, writes):
        n, v = event
        for k in reads:
            b = self.buf.setdefault(k, {"w": None, "r": {}})
            if b["r"].get(n, 0) < v:
                b["r"][n] = v
        for k in writes:
            self.buf[k] = {"w": event, "r": {}}

    def op(self, eng, fn, reads=(), writes=()):
        waits = self._deps(eng, reads, writes)
        sem = self._sem(eng)
        self.cnt[eng] += 1
        val = self.cnt[eng]

        def run(e, waits=waits, fn=fn, sem=sem):
            for s, v in waits:
                e.wait_ge(s, v)
            fn(e).then_inc(sem, 1)

        self.ops[eng].append(run)
        self._commit((eng, val), reads, writes)

    def dma(self, eng, out, in_, reads=(), writes=(), sem="dma", shared=False):
        self.dmas(eng, [(out, in_)], reads, writes, sem, shared)

    def dmas(self, eng, pairs, reads=(), writes=(), sem="dma", shared=False):
        waits = self._deps(eng, reads, writes)
        name = "d_" + sem
        s = self._sem(name)
        if shared:
            self.shared.add(name)
        self.cnt[name] += 16 * len(pairs)
        val = self.cnt[name]

        def run(e, waits=waits, pairs=pairs, s=s):
            for ws, v in waits:
                e.wait_ge(ws, v)
            for o, i in pairs:
                e.dma_start(out=o, in_=i).then_inc(s, 16)

        self.ops[eng].append(run)
        self._commit((name, val), reads, writes)

    def custom(self, eng, fn, reads=(), writes=(), sem="cc", inc=1):
        waits = self._deps(eng, reads, writes)
        s = self._sem(sem)
        self.cnt[sem] += inc
        val = self.cnt[sem]

        def run(e, waits=waits, fn=fn, s=s, inc=inc):
            for ws, v in waits:
                e.wait_ge(ws, v)
            fn(e).then_inc(s, inc)

        self.ops[eng].append(run)
        self._commit((sem, val), reads, writes)

    def barrier(self, engines=None, skip=()):
        tot = {k: v for k, v in self.cnt.items() if k not in skip}
        for eng in engines or self.ENG:
            waits = []
            for n, v in tot.items():
                if v and self.waited[eng].get(n, 0) < v and not (n == eng):
                    self.waited[eng][n] = v
                    waits.append((self.sem[n], v))

            def run(e, waits=waits):
                for s, v in waits:
                    e.wait_ge(s, v)

            self.ops[eng].append(run)

    def emit(self):
        ops = self.ops
        with self.nc.Block() as block:

            @block.tensor
            def _(e):
                for f in ops["pe"]:
                    f(e)

            @block.scalar
            def _(e):
                for f in ops["act"]:
                    f(e)

            @block.vector
            def _(e):
                for f in ops["dve"]:
                    f(e)

            @block.gpsimd
            def _(e):
                for f in ops["pool"]:
                    f(e)

            @block.sync
            def _(e):
                for f in ops["sp"]:
                    f(e)

        self.ops = {e: [] for e in self.ENG}


def mm_group(P, out_ps, pairs, reads, writes):
    def fn(e):
        n = len(pairs)
        ins = None
        for i, (l, r) in enumerate(pairs):
            ins = e.matmul(out_ps, l, r, start=(i == 0), stop=(i == n - 1))
        return ins

    P.op("pe", fn, reads, writes)


class WStream:
    def __init__(self, P, wst, wbf, tag):
        self.P, self.wst, self.wbf, self.tag = P, wst, wbf, tag
        self.blocks = []
        self.loaded = 0
        self.done = -1

    def add(self, src_ap):
        self.blocks.append(src_ap)
        return len(self.blocks) - 1

    def ensure(self, upto):
        P = self.P
        upto = min(upto, len(self.blocks) - 1)
        assert upto - (self.done + 1) < len(self.wbf), (upto, self.done)
        while self.loaded <= upto:
            n = self.loaded
            si = n % len(self.wst)
            bi = n % len(self.wbf)
            st, bf = self.wst[si], self.wbf[bi]
            P.dma("sp", st[:, :], self.blocks[n], writes=[(self.tag, "st", si)], sem=f"{self.tag}st{si}")
            P.op("act", lambda e, st=st, bf=bf: e.activation(out=bf[:, :], in_=st[:, :], func=AF.Copy),
                 reads=[(self.tag, "st", si)], writes=[(self.tag, "bf", bi)])
            self.loaded += 1

    def slot(self, n):
        bi = n % len(self.wbf)
        return self.wbf[bi], (self.tag, "bf", bi)


def rstd_from_ssq(P, ps_ap, ps_key, out_ap, out_key, n):
    P.op("act", lambda e: e.activation(out=out_ap, in_=ps_ap, func=AF.Sqrt, bias=EPS, scale=1.0 / n),
         reads=[ps_key], writes=[out_key])
    P.op("dve", lambda e: e.reciprocal(out=out_ap, in_=out_ap), reads=[out_key], writes=[out_key])


def phase1(P, nc, T, NST=2, after_st=None):
    P.buf = {}
    TS = TOK // NST
    NTT = TS // 512
    with ExitStack() as es:
        def sb(name, shape, dt):
            return es.enter_context(nc.sbuf_tensor("s_" + name, shape, dt))

        xres = sb("xres", [128, KT, TS], F32)
        hT = sb("hT", [128, KT, TS], BF16)
        yT = sb("yT", [128, KT, TS], BF16)
        xh = sb("xh", [128, KT, 2], F32)
        hTh = sb("hTh", [128, KT, 2], BF16)
        wst = [sb(f"wst{i}", [128, KT * 128], F32) for i in range(2)]
        wbf = [sb(f"wbf{i}", [128, KT * 128], BF16) for i in range(8)]
        n0w = sb("n0w", [128, KT], F32)
        n1w = sb("n1w", [128, KT], F32)
        wcv = sb("wcv", [128, KT, 3], F32)
        ones = sb("ones", [128, 128], F32)
        acc = sb("acc", [128, TS], F32)
        sq = [sb(f"sq{i}", [128, 512], F32) for i in range(2)]
        rstd = sb("rstd", [128, TS], F32)
        acch = sb("acch", [128, 2], F32)
        sqh = sb("sqh", [128, 2], F32)
        rstdh = sb("rstdh", [128, 2], F32)
        hal = sb("hal", [128, KT, 2], F32)
        uh = sb("uh", [128, 2], F32)
        cu = [sb(f"cu{i}", [128, 514], F32) for i in range(2)]
        tsz = [sb(f"tsz{i}", [128, 512], F32) for i in range(2)]
        tu = [sb(f"tu{i}", [128, 512], F32) for i in range(2)]
        tacc = sb("tacc", [128, 512], F32)
        tbs = sb("tbs", [128, 512], F32)
        ps = [es.enter_context(nc.psum_tensor(f"ps{i}", [128, 512], F32)) for i in range(8)]

        P.dma("sp", n0w[:, :], T["n0w"][:, :], writes=["n0w"], sem="const", shared=True)
        P.dma("sp", n1w[:, :], T["n1w"][:, :], writes=["n1w"], sem="const", shared=True)
        P.dma("sp", wcv[:, :, :], T["wcv"][:, :, :], writes=["wcv"], sem="const", shared=True)
        P.dma("sp", xh[:, :, :], T["xh"][:, :, :], writes=["xh"], sem="const", shared=True)
        P.op("pool", lambda e: e.memset(ones[:, :], 1.0), writes=["ones"])

        W = WStream(P, wst, wbf, "w")
        widx = {}
        for st in range(NST):
            for e_ in range(KT):
                for g in range(4):
                    widx[("in", st, e_, g)] = W.add(T["w_in"][e_ * 4 + g, :, :])
            for j in range(KT):
                widx[("out", st, j)] = W.add(T["w_out"][j, :, :])
        LA = 3
        W.ensure(3)

        for kt in range(KT):
            P.op("act", lambda e, kt=kt: e.activation(out=sqh[:, :], in_=xh[:, kt, :], func=AF.Square),
                 reads=["xh"], writes=["sqh"])
            if kt == 0:
                P.op("pool", lambda e: e.tensor_copy(out=acch[:, :], in_=sqh[:, :]), reads=["sqh"], writes=["acch"])
            else:
                P.op("pool", lambda e: e.tensor_tensor(out=acch[:, :], in0=acch[:, :], in1=sqh[:, :], op=ALU.add),
                     reads=["sqh", "acch"], writes=["acch"])
        mm_group(P, ps[0][:, 0:2], [(ones[:, :], acch[:, :])], reads=["ones", "acch"], writes=[("ps", 0)])
        rstd_from_ssq(P, ps[0][:, 0:2], ("ps", 0), rstdh[:, :], "rstdh", D)
        for kt in range(KT):
            P.op("dve", lambda e, kt=kt: e.scalar_tensor_tensor(
                out=hTh[:, kt, :], in0=xh[:, kt, :], scalar=n0w[:, kt:kt + 1], in1=rstdh[:, :],
                op0=ALU.mult, op1=ALU.mult), reads=["xh", "n0w", "rstdh"], writes=["hTh"])

        bank = [1]

        def next_bank():
            b = bank[0]
            bank[0] = (b + 1) % 8
            return b

        for st in range(NST):
            t0 = st * TS
            P.dmas("sp", [(xres[:, kt, :], T["xT"][kt * 128:(kt + 1) * 128, t0:t0 + TS]) for kt in range(KT)],
                   writes=[("xres", kt, tt) for kt in range(KT) for tt in range(NTT)], sem="x", shared=True)
            for tt in range(NTT):
                sl = slice(tt * 512, (tt + 1) * 512)
                b = next_bank()
                for kt in range(KT):
                    s_ = sq[kt % 2]
                    P.op("act", lambda e, sl=sl, kt=kt, s_=s_: e.activation(out=s_[:, :], in_=xres[:, kt, sl], func=AF.Square),
                         reads=[("xres", kt, tt)], writes=[("sq", kt % 2)])
                    P.op("pe", lambda e, b=b, kt=kt, s_=s_: e.matmul(ps[b][:, :], ones[:, :], s_[:, :], start=(kt == 0), stop=(kt == KT - 1)),
                         reads=["ones", ("sq", kt % 2)], writes=[("ps", b)])
                rstd_from_ssq(P, ps[b][:, :], ("ps", b), rstd[:, sl], ("rstd", tt), D)
                for kt in range(KT):
                    eng = "dve"
                    P.op(eng, lambda e, sl=sl, kt=kt: e.scalar_tensor_tensor(
                        out=hT[:, kt, sl], in0=xres[:, kt, sl], scalar=n0w[:, kt:kt + 1], in1=rstd[:, sl],
                        op0=ALU.mult, op1=ALU.mult),
                        reads=[("xres", kt, tt), "n0w", ("rstd", tt)], writes=[("hT", kt, tt)])

            it = 0
            for e_ in range(KT):
                wi = [widx[("in", st, e_, g)] for g in range(4)]
                W.ensure(wi[3] + 4)
                if st == 0:
                    bh = next_bank()
                    for g, off in ((1, 0), (2, 2)):
                        wb, wk = W.slot(wi[g])
                        mm_group(P, ps[bh][:, off:off + 2],
                                 [(wb[:, kt * 128:(kt + 1) * 128], hTh[:, kt, :]) for kt in range(KT)],
                                 reads=[wk, "hTh"], writes=[("ps", bh)])
                    P.op("act", lambda e, bh=bh: e.activation(out=uh[:, :], in_=ps[bh][:, 2:4], func=AF.Copy),
                         reads=[("ps", bh)], writes=["uh"])
                    P.op("dve", lambda e, bh=bh, e_=e_: e.tensor_tensor(out=hal[:, e_, :], in0=ps[bh][:, 0:2], in1=uh[:, :], op=ALU.mult),
                         reads=[("ps", bh), "uh"], writes=[("hal", e_)])
                for tt in range(NTT):
                    sl = slice(tt * 512, (tt + 1) * 512)
                    bs_ = [next_bank() for _ in range(4)]
                    for g in range(4):
                        wb, wk = W.slot(wi[g])
                        mm_group(P, ps[bs_[g]][:, :],
                                 [(wb[:, kt * 128:(kt + 1) * 128], hT[:, kt, sl]) for kt in range(KT)],
                                 reads=[wk] + [("hT", kt, tt) for kt in range(KT)], writes=[("ps", bs_[g])])
                    pB, pC, pU, pZ = [ps[b] for b in bs_]
                    kB, kC, kU, kZ = [("ps", b) for b in bs_]
                    i2 = it % 2
                    it += 1
                    cu_, sz_, u_ = cu[i2], tsz[i2], tu[i2]
                    P.op("act", lambda e, pZ=pZ, sz_=sz_: e.activation(out=sz_[:, :], in_=pZ[:, :], func=AF.Silu),
                         reads=[kZ], writes=[("tsz", i2)])
                    P.op("act", lambda e, pU=pU, u_=u_: e.activation(out=u_[:, :], in_=pU[:, :], func=AF.Copy),
                         reads=[kU], writes=[("tu", i2)])
                    P.op("pool", lambda e, cu_=cu_, e_=e_: e.tensor_copy(out=cu_[:, 0:2], in_=hal[:, e_, :]),
                         reads=[("hal", e_)], writes=[("cuh", i2)])
                    P.op("dve", lambda e, pC=pC, cu_=cu_, u_=u_: e.tensor_tensor(out=cu_[:, 2:514], in0=pC[:, :], in1=u_[:, :], op=ALU.mult),
                         reads=[kC, ("tu", i2)], writes=[("cu", i2)])
                    P.op("pool", lambda e, cu_=cu_, e_=e_: e.tensor_copy(out=hal[:, e_, :], in_=cu_[:, 512:514]),
                         reads=[("cu", i2)], writes=[("hal", e_)])
                    P.op("dve", lambda e, cu_=cu_, e_=e_: e.tensor_scalar_mul(out=tacc[:, :], in0=cu_[:, 2:514], scalar1=wcv[:, e_, 2:3]),
                         reads=[("cu", i2), "wcv"], writes=["tacc"])
                    P.op("dve", lambda e, cu_=cu_, e_=e_: e.scalar_tensor_tensor(
                        out=tacc[:, :], in0=cu_[:, 1:513], scalar=wcv[:, e_, 1:2], in1=tacc[:, :], op0=ALU.mult, op1=ALU.add),
                        reads=[("cu", i2), ("cuh", i2), "wcv", "tacc"], writes=["tacc"])
                    P.op("dve", lambda e, cu_=cu_, e_=e_: e.scalar_tensor_tensor(
                        out=tacc[:, :], in0=cu_[:, 0:512], scalar=wcv[:, e_, 0:1], in1=tacc[:, :], op0=ALU.mult, op1=ALU.add),
                        reads=[("cu", i2), ("cuh", i2), "wcv", "tacc"], writes=["tacc"])
                    P.op("dve", lambda e, pB=pB, sz_=sz_: e.tensor_tensor(out=tbs[:, :], in0=pB[:, :], in1=sz_[:, :], op=ALU.mult),
                         reads=[kB, ("tsz", i2)], writes=["tbs"])
                    P.op("pool", lambda e, e_=e_, sl=sl: e.tensor_tensor(out=yT[:, e_, sl], in0=tacc[:, :], in1=tbs[:, :], op=ALU.mult),
                         reads=["tacc", "tbs"], writes=[("yT", e_, tt)])
                W.done = wi[3]

            for j in range(KT):
                wi = widx[("out", st, j)]
                W.ensure(wi + 7)
                wb, wk = W.slot(wi)
                for tt in range(NTT):
                    sl = slice(tt * 512, (tt + 1) * 512)
                    b = next_bank()
                    mm_group(P, ps[b][:, :],
                             [(wb[:, kt * 128:(kt + 1) * 128], yT[:, kt, sl]) for kt in range(KT)],
                             reads=[wk] + [("yT", kt, tt) for kt in range(KT)], writes=[("ps", b)])
                    P.op("dve", lambda e, b=b, j=j, sl=sl: e.tensor_tensor(out=xres[:, j, sl], in0=ps[b][:, :], in1=xres[:, j, sl], op=ALU.add),
                         reads=[("ps", b), ("xres", j, tt)], writes=[("xres", j, tt)])
                    if j == 0:
                        P.op("act", lambda e, sl=sl: e.activation(out=acc[:, sl], in_=xres[:, 0, sl], func=AF.Square),
                             reads=[("xres", 0, tt)], writes=[("acc", tt)])
                    else:
                        s_ = sq[j % 2]
                        P.op("act", lambda e, sl=sl, j=j, s_=s_: e.activation(out=s_[:, :], in_=xres[:, j, sl], func=AF.Square),
                             reads=[("xres", j, tt)], writes=[("sq", j % 2)])
                        P.op("pool", lambda e, sl=sl, s_=s_: e.tensor_tensor(out=acc[:, sl], in0=acc[:, sl], in1=s_[:, :], op=ALU.add),
                             reads=[("sq", j % 2), ("acc", tt)], writes=[("acc", tt)])
                W.done = wi
                P.dma("act", T["x1T"][j * 128:(j + 1) * 128, t0:t0 + TS], xres[:, j, :],
                      reads=[("xres", j, tt) for tt in range(NTT)], writes=[("x1T", j, st)], sem="x1st", shared=True)

            for tt in range(NTT):
                sl = slice(tt * 512, (tt + 1) * 512)
                b = next_bank()
                mm_group(P, ps[b][:, :], [(ones[:, :], acc[:, sl])], reads=["ones", ("acc", tt)], writes=[("ps", b)])
                rstd_from_ssq(P, ps[b][:, :], ("ps", b), rstd[:, sl], ("rstd", tt), D)
                for kt in range(KT):
                    eng = "dve"
                    P.op(eng, lambda e, sl=sl, kt=kt: e.scalar_tensor_tensor(
                        out=hT[:, kt, sl], in0=xres[:, kt, sl], scalar=n1w[:, kt:kt + 1], in1=rstd[:, sl],
                        op0=ALU.mult, op1=ALU.mult),
                        reads=[("xres", kt, tt), "n1w", ("rstd", tt)], writes=[("hT", kt, tt)])
            for kt in range(KT):
                if after_st is not None:
                    dst = T["h1T"][st * D + kt * 128:st * D + (kt + 1) * 128, :]
                else:
                    dst = T["h1T"][kt * 128:(kt + 1) * 128, t0:t0 + TS]
                P.dma("act", dst, hT[:, kt, :],
                      reads=[("hT", kt, tt) for tt in range(NTT)], writes=[("h1T", kt, st)], sem="h1st", shared=True)
            if after_st is not None:
                after_st(st)

        P.barrier(skip=("cc",) if after_st is not None else ())
        P.emit()


def transpose_op(P, out_ps, in_ap, ident, reads, writes):
    P.op("pe", lambda e: e.transpose(out_ps, in_ap, ident), reads, writes)


def phase2(P, nc, T, NTOK=SEQ, seg_out=False, after_seg=None):
    P.buf = {k: v for k, v in P.buf.items() if isinstance(k, tuple) and k[0] in ("h1all", "yall")}
    NG = NTOK // 512
    with ExitStack() as es:
        def sb(name, shape, dt):
            return es.enter_context(nc.sbuf_tensor("s2_" + name, shape, dt))

        wqk = sb("wqk", [128, KT, 512], BF16)
        wv = sb("wv", [128, KT, 512], BF16)
        wg = sb("wg", [128, KT, 512], BF16)
        wgk = sb("wgk", [128, KT, 16], BF16)
        stg = [sb(f"stg{i}", [128, 2048], F32) for i in range(2)]
        wgk2a = sb("wgk2a", [17, 256], F32)
        gnw = sb("gnw", [128, 512], F32)
        umask = sb("umask", [128, 128], F32)
        tri16 = sb("tri16", [128, 128], F32)
        ident = sb("ident", [128, 128], BF16)
        hg = [sb(f"hg{i}", [128, KT, 512], BF16) for i in range(2)]
        qk = [sb(f"qk{i}", [128, 4, 512], F32) for i in range(2)]
        gka = [sb(f"gka{i}", [32, 512], F32) for i in range(2)]
        S = sb("S", [128, 2, 512], F32)
        Sb = sb("Sb", [128, 2, 512], BF16)
        vb = [sb(f"vb{i}", [128, 512], BF16) for i in range(2)]
        sg = [sb(f"sg{i}", [128, 512], F32) for i in range(2)]
        m_ = sb("m", [128, 256], F32)
        na = sb("na", [128, 256], F32)
        ex = sb("ex", [128, 256], F32)
        ls = sb("ls", [128, 256], F32)
        eb = [sb(f"eb{i}", [128, 256], F32) for i in range(2)]
        enb = sb("enb", [128, 256], F32)
        qt = [sb(f"qt{i}", [128, 2, 128], BF16) for i in range(2)]
        kt_ = [sb(f"kt{i}", [128, 2, 128], BF16) for i in range(2)]
        At = sb("At", [128, 128], BF16)
        ktok = sb("ktok", [128, 256], BF16)
        junk = sb("junk", [128, 512], F32)
        ssq = sb("ssq", [128, 1], F32)
        rs = sb("rs", [128, 1], F32)
        epsb = sb("epsb", [128, 1], F32)
        oneb = sb("oneb", [128, 1], F32)
        y1 = sb("y1", [128, 512], F32)
        yb = sb("yb", [128, 512], BF16)
        yTs = [sb(f"yTs{i}", [128, 4, 512], BF16) for i in range(2)]
        ps = [es.enter_context(nc.psum_tensor(f"p2s{i}", [128, 512], F32)) for i in range(7)]
        pst = es.enter_context(nc.psum_tensor("p2t", [128, 1024], BF16))

        bank = [0]

        def nb():
            b = bank[0]
            bank[0] = (b + 1) % 7
            return b

        for name, dst, src in (("wgk2a", wgk2a, T["wgk2a"]), ("gnw", gnw, T["gnw"]), ("umask", umask, T["umask"]),
                               ("tri16", tri16, T["tri16"]), ("ident", ident, T["ident"])):
            P.dma("sp", dst[:, :], src[:, :], writes=[name], sem="c2", shared=True)
        for i in range(2):
            P.op("pool", lambda e, i=i: e.memset(gka[i][:, :], 1.0), writes=[("gka", i)])
        P.op("pool", lambda e: e.memset(epsb[:, :], EPS), writes=["epsb"])
        P.op("pool", lambda e: e.memset(oneb[:, :], 1.0), writes=["oneb"])
        P.op("pool", lambda e: e.memset(S[:, :, :], 0.0), writes=[("S", 0), ("S", 1)])
        P.op("pool", lambda e: e.memset(Sb[:, :, :], 0.0), writes=[("Sb", 0), ("Sb", 1)])
        n = 0
        for name, dst, src in (("wqk", wqk, T["wqk"]), ("wv", wv, T["wv"]), ("wg", wg, T["wg"])):
            for c in range(0, KT, 4):
                w_ = min(4, KT - c)
                si = n % 2
                n += 1
                P.dma("sp", stg[si][:, 0:w_ * 512], src[:, c * 512:(c + w_) * 512], writes=[("stg", si)], sem=f"stg{si}")
                if n % 2 == 0:
                    P.op("dve", lambda e, dst=dst, c=c, w_=w_, si=si: e.tensor_copy(
                        out=dst[:, c:c + w_, :], in_=stg[si][:, 0:w_ * 512].rearrange("p (k c) -> p k c", c=512)),
                        reads=[("stg", si)], writes=[(name, c)])
                else:
                    P.op("act", lambda e, dst=dst, c=c, w_=w_, si=si: e.activation(
                        out=dst[:, c:c + w_, :], in_=stg[si][:, 0:w_ * 512].rearrange("p (k c) -> p k c", c=512), func=AF.Copy),
                        reads=[("stg", si)], writes=[(name, c)])
        si = n % 2
        P.dma("sp", stg[si][:, 0:KT * 16], T["wgk"][:, :], writes=[("stg", si)], sem=f"stg{si}")
        P.op("pool", lambda e, si=si: e.tensor_copy(out=wgk[:, :, :], in_=stg[si][:, 0:KT * 16].rearrange("p (k c) -> p k c", c=16)),
             reads=[("stg", si)], writes=["wgk"])
        wkeys = [("wqk", c) for c in range(0, KT, 4)]
        vkeys = [("wv", c) for c in range(0, KT, 4)]
        gkeys = [("wg", c) for c in range(0, KT, 4)]

        NCH = NG * 4
        At2 = [At, sb("At1", [128, 128], BF16)]
        ktok2 = [ktok, sb("ktok1", [128, 256], BF16)]

        def seg_off(g):
            return divmod(g * 512, TOK)

        def load_group(g):
            seg, off = seg_off(g)
            hg_ = hg[g % 2]
            if seg_out:
                st_, col = divmod(off, 1024)
                srcs = [T["h1"][(st_ * 8 + kt // 2) * 1024 + seg * 256 + (kt % 2) * 128:(st_ * 8 + kt // 2) * 1024 + seg * 256 + (kt % 2) * 128 + 128,
                                col:col + 512] for kt in range(KT)]
            else:
                srcs = [T["h1"][seg * D + kt * 128: seg * D + (kt + 1) * 128, off:off + 512] for kt in range(KT)]
            rd = [("h1all", st_, kp) for kp in range(8)] if seg_out else []
            P.dmas("sp", [(hg_[:, kt, :], srcs[kt]) for kt in range(KT)], reads=rd, writes=[("hg", g % 2)], sem=f"hg{g % 2}")

        def proj_group(g):
            hg_, hk, qk_ = hg[g % 2], ("hg", g % 2), qk[g % 2]
            for dt in range(4):
                b = nb()
                mm_group(P, ps[b][:, :], [(wqk[:, kt, dt * 128:(dt + 1) * 128], hg_[:, kt, :]) for kt in range(KT)],
                         reads=wkeys + [hk], writes=[("ps", b)])
                P.op("act", lambda e, b=b, dt=dt, qk_=qk_: e.activation(out=qk_[:, dt, :], in_=ps[b][:, :], func=AF.Copy),
                     reads=[("ps", b)], writes=[("qk", g % 2, dt)])
            b = nb()
            mm_group(P, ps[b][0:16, :], [(wgk[:, kt, :], hg_[:, kt, :]) for kt in range(KT)], reads=["wgk", hk], writes=[("ps", b)])
            gka_ = gka[g % 2]
            P.op("act", lambda e, b=b, gka_=gka_: e.activation(out=gka_[0:16, :], in_=ps[b][0:16, :], func=AF.Copy),
                 reads=[("ps", b)], writes=[("gka", g % 2)])

        def ctx(n):
            g, c = divmod(n, 4)
            return g, c, n % 2, slice(c * 128, (c + 1) * 128), hg[g % 2], ("hg", g % 2), qk[g % 2], gka[g % 2]

        A = {}

        def stageA1(n):
            g, c, i2, cs, hg_, hk, qk_, gka_ = ctx(n)
            bl = nb()
            mm_group(P, ps[bl][:, 0:256], [(gka_[0:17, cs], wgk2a[0:17, :])], reads=[("gka", g % 2), "wgk2a"], writes=[("ps", bl)])
            bv = nb()
            mm_group(P, ps[bv][:, :], [(hg_[:, kt, cs], wv[:, kt, :]) for kt in range(KT)], reads=vkeys + [hk], writes=[("ps", bv)])
            P.op("dve", lambda e, bl=bl: e.tensor_scalar_min(out=m_[:, :], in0=ps[bl][:, 0:256], scalar1=0.0),
                 reads=[("ps", bl)], writes=["m"])
            P.op("dve", lambda e, bl=bl: e.scalar_tensor_tensor(out=na[:, :], in0=m_[:, :], scalar=2.0, in1=ps[bl][:, 0:256],
                                                                 op0=ALU.mult, op1=ALU.subtract),
                 reads=[("ps", bl), "m"], writes=["na"])
            P.op("act", lambda e: e.activation(out=ex[:, :], in_=na[:, :], func=AF.Exp), reads=["na"], writes=["ex"])
            P.op("act", lambda e: e.activation(out=ex[:, :], in_=ex[:, :], func=AF.Ln, bias=1.0), reads=["ex"], writes=["ex"])
            P.op("dve", lambda e: e.tensor_tensor(out=ls[:, :], in0=m_[:, :], in1=ex[:, :], op=ALU.subtract),
                 reads=["m", "ex"], writes=["ls"])
            P.op("act", lambda e, bv=bv, i2=i2: e.activation(out=vb[i2][:, :], in_=ps[bv][:, :], func=AF.Copy),
                 reads=[("ps", bv)], writes=[("vb", i2)])

        def stageA2(n):
            g, c, i2, cs, hg_, hk, qk_, gka_ = ctx(n)
            bg = nb()
            mm_group(P, ps[bg][:, :], [(hg_[:, kt, cs], wg[:, kt, :]) for kt in range(KT)], reads=gkeys + [hk], writes=[("ps", bg)])
            P.op("act", lambda e, bg=bg, i2=i2: e.activation(out=sg[i2][:, :], in_=ps[bg][:, :], func=AF.Exp, scale=-1.0),
                 reads=[("ps", bg)], writes=[("sg", i2)])
            P.op("act", lambda e, i2=i2: e.activation(out=sg[i2][:, :], in_=sg[i2][:, :], func=AF.Copy, bias=1.0),
                 reads=[("sg", i2)], writes=[("sg", i2)])
            P.op("dve", lambda e, i2=i2: e.reciprocal(out=sg[i2][:, :], in_=sg[i2][:, :]), reads=[("sg", i2)], writes=[("sg", i2)])
            P.op("dve", lambda e, bg=bg, i2=i2: e.tensor_tensor(out=sg[i2][:, :], in0=ps[bg][:, :], in1=sg[i2][:, :], op=ALU.mult),
                 reads=[("ps", bg), ("sg", i2)], writes=[("sg", i2)])

        def stageA3(n):
            g, c, i2, cs, hg_, hk, qk_, gka_ = ctx(n)
            bb = nb()
            for dt in range(2):
                mm_group(P, ps[bb][:, dt * 128:(dt + 1) * 128], [(ls[:, dt * 128:(dt + 1) * 128], tri16[:, :])],
                         reads=["ls", "tri16"], writes=[("ps", bb)])
            eb_ = eb[i2]
            P.op("act", lambda e, bb=bb, eb_=eb_: e.activation(out=eb_[:, :], in_=ps[bb][:, 0:256], func=AF.Exp),
                 reads=[("ps", bb)], writes=[("eb", i2)])
            P.op("act", lambda e, bb=bb: e.activation(out=enb[:, :], in_=ps[bb][:, 0:256], func=AF.Exp, scale=-1.0),
                 reads=[("ps", bb)], writes=["enb"])
            qt_, kk_ = qt[i2], kt_[i2]
            for dt in range(2):
                P.op("dve", lambda e, dt=dt, qt_=qt_, eb_=eb_, qk_=qk_, cs=cs: e.scalar_tensor_tensor(
                    out=qt_[:, dt, :], in0=qk_[:, dt, cs], scalar=HK ** -0.5, in1=eb_[:, dt * 128:(dt + 1) * 128],
                    op0=ALU.mult, op1=ALU.mult), reads=[("qk", g % 2, dt), ("eb", i2)], writes=[("qt", i2, dt)])
                P.op("dve", lambda e, dt=dt, kk_=kk_, qk_=qk_, cs=cs: e.tensor_tensor(
                    out=kk_[:, dt, :], in0=qk_[:, 2 + dt, cs], in1=enb[:, dt * 128:(dt + 1) * 128], op=ALU.mult),
                    reads=[("qk", g % 2, 2 + dt), "enb"], writes=[("kt", i2, dt)])

        def stageA4(n):
            g, c, i2, cs, hg_, hk, qk_, gka_ = ctx(n)
            qt_, kk_ = qt[i2], kt_[i2]
            ba = nb()
            mm_group(P, ps[ba][:, 0:128], [(kk_[:, dt, :], qt_[:, dt, :]) for dt in range(2)],
                     reads=[("kt", i2, 0), ("kt", i2, 1), ("qt", i2, 0), ("qt", i2, 1)], writes=[("ps", ba)])
            for dt in range(2):
                transpose_op(P, pst[:, dt * 128:(dt + 1) * 128], kk_[:, dt, :], ident[:, :],
                             reads=[("kt", i2, dt), "ident"], writes=["pst"])
            P.op("dve", lambda e, ba=ba, i2=i2: e.tensor_tensor(out=At2[i2][:, :], in0=ps[ba][:, 0:128], in1=umask[:, :], op=ALU.mult),
                 reads=[("ps", ba), "umask"], writes=[("At", i2)])
            P.op("act", lambda e, i2=i2: e.activation(out=ktok2[i2][:, :], in_=pst[:, 0:256], func=AF.Copy),
                 reads=["pst"], writes=[("ktok", i2)])

        def stageA(n):
            stageA1(n)
            stageA2(n)
            stageA3(n)
            stageA4(n)

        def stageB1(n):
            i2 = n % 2
            qt_, eb_ = qt[i2], eb[i2]
            bo = nb()
            A[n] = bo
            mm_group(P, ps[bo][:, :], [(At2[i2][:, :], vb[i2][:, :]), (qt_[:, 0, :], Sb[:, 0, :]), (qt_[:, 1, :], Sb[:, 1, :])],
                     reads=[("At", i2), ("vb", i2), ("qt", i2, 0), ("qt", i2, 1), ("Sb", 0), ("Sb", 1)], writes=[("ps", bo)])
            for dt in range(2):
                bs_ = nb()
                mm_group(P, ps[bs_][:, :], [(ktok2[i2][:, dt * 128:(dt + 1) * 128], vb[i2][:, :])],
                         reads=[("ktok", i2), ("vb", i2)], writes=[("ps", bs_)])
                P.op("dve", lambda e, bs_=bs_, dt=dt: e.tensor_tensor(out=S[:, dt, :], in0=ps[bs_][:, :], in1=S[:, dt, :], op=ALU.add),
                     reads=[("ps", bs_), ("S", dt)], writes=[("S", dt)])
                P.op("act", lambda e, dt=dt, eb_=eb_: e.activation(out=S[:, dt, :], in_=S[:, dt, :], func=AF.Copy,
                                                                    scale=eb_[:, dt * 128 + 127:dt * 128 + 128]),
                     reads=[("S", dt), ("eb", i2)], writes=[("S", dt)])
                P.op("pool", lambda e, dt=dt: e.tensor_copy(out=Sb[:, dt, :], in_=S[:, dt, :]), reads=[("S", dt)], writes=[("Sb", dt)])

        def stageB2(n):
            i2 = n % 2
            bo = A.pop(n)
            P.op("pool", lambda e: e.memset(ssq[:, :], 0.0), writes=["ssq"])
            P.op("act", lambda e, bo=bo: e.activation(out=junk[:, :], in_=ps[bo][:, :], func=AF.Square, accum_out=ssq[:, :]),
                 reads=[("ps", bo)], writes=["junk", "ssq"])
            P.op("act", lambda e: e.activation(out=rs[:, :], in_=ssq[:, :], func=AF.Ln, bias=epsb[:, :], scale=1.0 / HV),
                 reads=["ssq", "epsb"], writes=["rs"])
            P.op("act", lambda e: e.activation(out=rs[:, :], in_=rs[:, :], func=AF.Exp, scale=-0.5), reads=["rs"], writes=["rs"])
            P.op("dve", lambda e, bo=bo: e.scalar_tensor_tensor(out=y1[:, :], in0=ps[bo][:, :], scalar=rs[:, 0:1], in1=gnw[:, :],
                                                                 op0=ALU.mult, op1=ALU.mult),
                 reads=[("ps", bo), "rs", "gnw"], writes=["y1"])
            P.op("dve", lambda e, i2=i2: e.tensor_tensor(out=yb[:, :], in0=y1[:, :], in1=sg[i2][:, :], op=ALU.mult),
                 reads=["y1", ("sg", i2)], writes=["yb"])

        def stageC(n):
            g, c = divmod(n, 4)
            cs = slice(c * 128, (c + 1) * 128)
            yT_ = yTs[g % 2]
            for et in range(4):
                transpose_op(P, pst[:, 512 + et * 128:512 + (et + 1) * 128], yb[:, et * 128:(et + 1) * 128], ident[:, :],
                             reads=["yb", "ident"], writes=["pst"])
            P.op("act", lambda e, yT_=yT_, cs=cs: e.activation(
                out=yT_[:, :, cs], in_=pst[:, 512:1024].rearrange("p (e t) -> p e t", t=128), func=AF.Copy),
                reads=["pst"], writes=[("yTs", g % 2, c)])
            if c == 3:
                seg, off = seg_off(g)
                if seg_out:
                    half, col = divmod(off, 1024)
                    dsts = [T["yseg"][((seg * 2 + half) * 4 + et) * 128:((seg * 2 + half) * 4 + et + 1) * 128, col:col + 512] for et in range(4)]
                else:
                    dsts = [T["yT"][et * 128:(et + 1) * 128, g * 512:(g + 1) * 512] for et in range(4)]
                P.dmas("act", [(dsts[et], yT_[:, et, :]) for et in range(4)],
                       reads=[("yTs", g % 2, cc_) for cc_ in range(4)], writes=[("yTd", g)], sem=f"yst{g % 2}")
                if after_seg is not None and g % 2 == 1:
                    after_seg(g // 2)

        load_group(0)
        proj_group(0)
        if NG > 1:
            load_group(1)
        stageA(0)
        for n in range(NCH):
            nxt = n + 1 < NCH
            if nxt:
                g1, c1 = divmod(n + 1, 4)
                if c1 == 0:
                    proj_group(g1)
                    if g1 + 1 < NG:
                        load_group(g1 + 1)
                stageA1(n + 1)
                stageA2(n + 1)
                stageA3(n + 1)
            if n >= 1:
                stageC(n - 1)
            stageB1(n)
            if nxt:
                stageA4(n + 1)
            stageB2(n)
        stageC(NCH - 1)
        P.barrier(skip=("cc",) if after_seg is not None else ())
        P.emit()


def phase3(P, nc, T, indirect=False):
    P.buf = {k: v for k, v in P.buf.items() if isinstance(k, tuple) and k[0] in ("h1all", "yall")}
    NTT = TOK // 512
    with ExitStack() as es:
        def sb(name, shape, dt):
            return es.enter_context(nc.sbuf_tensor("s3_" + name, shape, dt))

        yTs = sb("yT", [128, KT, TOK], BF16)
        wo = [sb(f"wo{j}", [128, KT * 128], BF16) for j in range(KT)]
        stg = [sb(f"stg{i}", [128, KT * 128], F32) for i in range(2)]
        x2 = sb("x2", [128, KT, 512], F32)
        nfw = sb("nfw", [128, KT], F32)
        ones = sb("ones", [128, 128], F32)
        acc = sb("acc", [128, 512], F32)
        sq = [sb(f"sq{i}", [128, 512], F32) for i in range(2)]
        rstd = sb("rstd", [128, 512], F32)
        ps = [es.enter_context(nc.psum_tensor(f"p3s{i}", [128, 512], F32)) for i in range(8)]
        P.dma("sp", nfw[:, :], T["nfw"][:, :], writes=["nfw"], sem="c3", shared=True)
        P.op("pool", lambda e: e.memset(ones[:, :], 1.0), writes=["ones"])
        if indirect:
            ix = es.enter_context(nc.sbuf_tensor("s3_ix", [128, 2 * KT], mybir.dt.int32))
            P.dma("sp", ix[:, :], T["yidx"][:, :], writes=["ix"], sem="c3", shared=True)
            for half in range(2):
                for kt in range(KT):
                    q = half * KT + kt
                    P.custom("pool", lambda e, kt=kt, half=half, q=q: e.indirect_dma_start(
                        out=yTs[:, kt, half * 1024:(half + 1) * 1024], out_offset=None, in_=T["yall"][:, :],
                        in_offset=bass.IndirectOffsetOnAxis(ap=ix[:, q:q + 1], axis=0)),
                        reads=["ix"] + [("yall", c) for c in range(32)], writes=["yT"] if q == 2 * KT - 1 else [("yTpart", q)],
                        sem="d_yind", inc=16)
            P.shared.add("d_yind")
        for j in range(KT):
            si = j % 2
            P.dma("sp", stg[si][:, :], T["w_out"][j, :, :], writes=[("stg", si)], sem=f"stg{si}")
            if j % 2 == 0:
                P.op("dve", lambda e, j=j, si=si: e.tensor_copy(out=wo[j][:, :], in_=stg[si][:, :]), reads=[("stg", si)], writes=[("wo", j)])
            else:
                P.op("act", lambda e, j=j, si=si: e.activation(out=wo[j][:, :], in_=stg[si][:, :], func=AF.Copy),
                     reads=[("stg", si)], writes=[("wo", j)])
            if j == 1 and not indirect:
                P.dmas("sp", [(yTs[:, kt, :], T["yT"][kt * 128:(kt + 1) * 128, :]) for kt in range(KT)],
                       writes=["yT"], sem="y3", shared=True)
        bank = [0]
        for tt in range(NTT):
            sl = slice(tt * 512, (tt + 1) * 512)
            for j in range(KT):
                P.dma("sp", x2[:, j, :], T["x1T"][j * 128:(j + 1) * 128, sl], writes=[("x2", j)], sem=f"x3_{j}")
            for j in range(KT):
                b = bank[0]
                bank[0] = (b + 1) % 8
                mm_group(P, ps[b][:, :], [(wo[j][:, kt * 128:(kt + 1) * 128], yTs[:, kt, sl]) for kt in range(KT)],
                         reads=[("wo", j), "yT"], writes=[("ps", b)])
                P.op("dve", lambda e, b=b, j=j: e.tensor_tensor(out=x2[:, j, :], in0=ps[b][:, :], in1=x2[:, j, :], op=ALU.add),
                     reads=[("ps", b), ("x2", j)], writes=[("x2", j)])
                if j == 0:
                    P.op("act", lambda e: e.activation(out=acc[:, :], in_=x2[:, 0, :], func=AF.Square), reads=[("x2", 0)], writes=["acc"])
                else:
                    s_ = sq[j % 2]
                    P.op("act", lambda e, j=j, s_=s_: e.activation(out=s_[:, :], in_=x2[:, j, :], func=AF.Square),
                         reads=[("x2", j)], writes=[("sq", j % 2)])
                    P.op("pool", lambda e, s_=s_: e.tensor_tensor(out=acc[:, :], in0=acc[:, :], in1=s_[:, :], op=ALU.add),
                         reads=[("sq", j % 2), "acc"], writes=["acc"])
            b = bank[0]
            bank[0] = (b + 1) % 8
            mm_group(P, ps[b][:, :], [(ones[:, :], acc[:, :])], reads=["ones", "acc"], writes=[("ps", b)])
            rstd_from_ssq(P, ps[b][:, :], ("ps", b), rstd[:, :], "rstd", D)
            for j in range(KT):
                P.op("dve", lambda e, j=j: e.scalar_tensor_tensor(out=x2[:, j, :], in0=x2[:, j, :], scalar=nfw[:, j:j + 1], in1=rstd[:, :],
                                                                   op0=ALU.mult, op1=ALU.mult),
                     reads=[("x2", j), "nfw", "rstd"], writes=[("x2", j)])
            for j in range(KT):
                P.dma("act", T["oT"][j * 128:(j + 1) * 128, sl], x2[:, j, :], reads=[("x2", j)], writes=[("oT", tt, j)], sem=f"o3_{j}")
        P.barrier()
        P.emit()


def new_nc():
    nc = bass.Bass("TRN2", target_bir_lowering=False)
    return nc


def build_p1():
    nc = new_nc()
    T = {}
    T["xT"] = nc.dram_tensor("xT", [D, TOK], F32, kind="ExternalInput")
    T["xh"] = nc.dram_tensor("xh", [128, KT, 2], F32, kind="ExternalInput")
    T["n0w"] = nc.dram_tensor("n0w", [128, KT], F32, kind="ExternalInput")
    T["n1w"] = nc.dram_tensor("n1w", [128, KT], F32, kind="ExternalInput")
    T["wcv"] = nc.dram_tensor("wcv", [128, KT, 3], F32, kind="ExternalInput")
    T["w_in"] = nc.dram_tensor("w_in", [4 * KT, 128, KT * 128], F32, kind="ExternalInput")
    T["w_out"] = nc.dram_tensor("w_out", [KT, 128, KT * 128], F32, kind="ExternalInput")
    T["x1T"] = nc.dram_tensor("x1T", [D, TOK], F32, kind="ExternalOutput")
    T["h1T"] = nc.dram_tensor("h1T", [D, TOK], BF16, kind="ExternalOutput")
    with ExitStack() as es:
        P = Prog(nc, es)
        phase1(P, nc, T)
    return nc


def vec_pk(w):
    return np.ascontiguousarray(w.reshape(KT, 128).T)


def wblocks(w, ncols_blk=128):
    K, N = w.shape
    nb = N // 128
    a = w.reshape(KT, 128, nb, 128)
    a = a.transpose(2, 1, 0, 3)
    return np.ascontiguousarray(a.reshape(nb, 128, KT * 128))


def p1_inputs(x, norm0_w, conv0_w_in, conv0_w_conv, conv0_w_out, norm1_w):
    E = D
    wb = wblocks(conv0_w_in)
    wb = wb.reshape(4, 16, 128, 2048).transpose(1, 0, 2, 3).reshape(64, 128, 2048)
    wb = np.ascontiguousarray(wb)
    wo = wblocks(conv0_w_out)
    wcv = np.ascontiguousarray(conv0_w_conv.reshape(KT, 128, 3).transpose(1, 0, 2))
    n0 = vec_pk(norm0_w)
    n1 = vec_pk(norm1_w)
    maps = []
    for c in range(NCORES):
        b, s = divmod(c, 4)
        xs = x[b, s * TOK:(s + 1) * TOK, :]
        xT = np.ascontiguousarray(xs.T)
        if s == 0:
            hal = np.zeros((2, D), np.float32)
        else:
            hal = x[b, s * TOK - 2:s * TOK, :]
        xh = np.ascontiguousarray(hal.T.reshape(KT, 128, 2).transpose(1, 0, 2))
        maps.append({"xT": xT, "xh": xh, "n0w": n0, "n1w": n1, "wcv": wcv, "w_in": wb, "w_out": wo})
    return maps


def build_p2(ntok=SEQ):
    nc = new_nc()
    T = {}
    T["h1"] = nc.dram_tensor("h1", [(ntok // TOK) * D, TOK], BF16, kind="ExternalInput")
    T["wqk"] = nc.dram_tensor("wqk", [128, KT * 512], F32, kind="ExternalInput")
    T["wv"] = nc.dram_tensor("wv", [128, KT * 512], F32, kind="ExternalInput")
    T["wg"] = nc.dram_tensor("wg", [128, KT * 512], F32, kind="ExternalInput")
    T["wgk"] = nc.dram_tensor("wgk", [128, KT * 16], F32, kind="ExternalInput")
    T["wgk2a"] = nc.dram_tensor("wgk2a", [17, 256], F32, kind="ExternalInput")
    T["gnw"] = nc.dram_tensor("gnw", [128, 512], F32, kind="ExternalInput")
    T["umask"] = nc.dram_tensor("umask", [128, 128], F32, kind="ExternalInput")
    T["tri16"] = nc.dram_tensor("tri16", [128, 128], F32, kind="ExternalInput")
    T["ident"] = nc.dram_tensor("ident", [128, 128], BF16, kind="ExternalInput")
    T["yT"] = nc.dram_tensor("yT", [HV, ntok], BF16, kind="ExternalOutput")
    with ExitStack() as es:
        P = Prog(nc, es)
        phase2(P, nc, T, ntok)
    return nc


def build_p3():
    nc = new_nc()
    T = {}
    T["yT"] = nc.dram_tensor("yT", [D, TOK], BF16, kind="ExternalInput")
    T["x1T"] = nc.dram_tensor("x1T", [D, TOK], F32, kind="ExternalInput")
    T["w_out"] = nc.dram_tensor("w_out", [KT, 128, KT * 128], F32, kind="ExternalInput")
    T["nfw"] = nc.dram_tensor("nfw", [128, KT], F32, kind="ExternalInput")
    T["oT"] = nc.dram_tensor("oT", [D, TOK], F32, kind="ExternalOutput")
    with ExitStack() as es:
        P = Prog(nc, es)
        phase3(P, nc, T)
    return nc


def pk_cols(w):
    C = w.shape[1]
    return np.ascontiguousarray(w.reshape(KT, 128, C).transpose(1, 0, 2).reshape(128, KT * C))


def p2_consts():
    jj, ii = np.meshgrid(np.arange(128), np.arange(128), indexing="ij")
    um = (jj <= ii).astype(np.float32)
    return {"umask": um, "tri16": (um / 16.0).astype(np.float32), "ident": np.eye(128, dtype=np.float32).astype(ml_dtypes.bfloat16)}


def p2_weights(h, gla1_w_in, gla1_w_gk2, gla1_b_gk2, gla1_gn_w):
    KD, VD = 4 * HK, 4 * HV
    wq = gla1_w_in[:, h * HK:(h + 1) * HK]
    wk = gla1_w_in[:, KD + h * HK:KD + (h + 1) * HK]
    wv = gla1_w_in[:, 2 * KD + h * HV:2 * KD + (h + 1) * HV]
    wg = gla1_w_in[:, 2 * KD + VD + h * HV:2 * KD + VD + (h + 1) * HV]
    wgk = gla1_w_in[:, 2 * KD + 2 * VD:]
    m = {"wqk": pk_cols(np.concatenate([wq, wk], axis=1)), "wv": pk_cols(wv), "wg": pk_cols(wg), "wgk": pk_cols(wgk),
         "wgk2a": np.ascontiguousarray(np.concatenate([gla1_w_gk2[:, h * HK:(h + 1) * HK], gla1_b_gk2[None, h * HK:(h + 1) * HK]], axis=0)),
         "gnw": np.ascontiguousarray(np.broadcast_to(gla1_gn_w[None, :], (128, HV)))}
    m.update(p2_consts())
    return m


def build_fused():
    nc = new_nc()
    EI = dict(kind="ExternalInput")
    T1 = {"xT": nc.dram_tensor("xT", [D, TOK], F32, **EI), "xh": nc.dram_tensor("xh", [128, KT, 2], F32, **EI),
          "n0w": nc.dram_tensor("n0w", [128, KT], F32, **EI), "n1w": nc.dram_tensor("n1w", [128, KT], F32, **EI),
          "wcv": nc.dram_tensor("wcv", [128, KT, 3], F32, **EI),
          "w_in": nc.dram_tensor("w_in", [4 * KT, 128, KT * 128], F32, **EI),
          "w_out": nc.dram_tensor("w_out0", [KT, 128, KT * 128], F32, **EI),
          "x1T": nc.dram_tensor("x1T_d", [D, TOK], F32), "h1T": nc.dram_tensor("h1T_d", [2 * D, TOK // 2], BF16)}
    h1all = nc.dram_tensor("h1all_d", [16 * 1024, TOK // 2], BF16)
    T2 = {"h1": h1all, "wqk": nc.dram_tensor("wqk", [128, KT * 512], F32, **EI), "wv": nc.dram_tensor("wv", [128, KT * 512], F32, **EI),
          "wg": nc.dram_tensor("wg", [128, KT * 512], F32, **EI), "wgk": nc.dram_tensor("wgk", [128, KT * 16], F32, **EI),
          "wgk2a": nc.dram_tensor("wgk2a", [17, 256], F32, **EI), "gnw": nc.dram_tensor("gnw", [128, 512], F32, **EI),
          "umask": nc.dram_tensor("umask", [128, 128], F32, **EI), "tri16": nc.dram_tensor("tri16", [128, 128], F32, **EI),
          "ident": nc.dram_tensor("ident", [128, 128], BF16, **EI), "yseg": nc.dram_tensor("yseg_d", [32 * 128, TOK // 2], BF16)}
    yall = nc.dram_tensor("yall_d", [32 * 512, TOK // 2], BF16)
    T3 = {"yall": yall, "yidx": nc.dram_tensor("yidx", [128, 2 * KT], mybir.dt.int32, **EI), "x1T": T1["x1T"],
          "w_out": nc.dram_tensor("w_out1", [KT, 128, KT * 128], F32, **EI), "nfw": nc.dram_tensor("nfw", [128, KT], F32, **EI),
          "oT": nc.dram_tensor("oT", [D, TOK], F32, kind="ExternalOutput")}
    groups = [[0, 1, 2, 3], [4, 5, 6, 7]]
    with ExitStack() as es:
        P = Prog(nc, es)
        def ag1(st):
            for kp in range(8):
                P.custom("pool", lambda e, st=st, kp=kp: e.collective_compute(
                    "AllGather", ALU.bypass, replica_groups=groups,
                    ins=[T1["h1T"][st * D + kp * 256:st * D + (kp + 1) * 256, :].opt()],
                    outs=[h1all[(st * 8 + kp) * 1024:(st * 8 + kp + 1) * 1024, :].opt()]),
                    reads=[("h1T", 2 * kp, st), ("h1T", 2 * kp + 1, st)], writes=[("h1all", st, kp)], sem="cc", inc=1)

        def ag2(hidx):
            for et in range(4):
                c = hidx * 4 + et
                P.custom("pool", lambda e, c=c: e.collective_compute(
                    "AllGather", ALU.bypass, replica_groups=groups,
                    ins=[T2["yseg"][c * 128:(c + 1) * 128, :].opt()], outs=[yall[c * 512:(c + 1) * 512, :].opt()]),
                    reads=[("yTd", 2 * hidx), ("yTd", 2 * hidx + 1)], writes=[("yall", c)], sem="cc", inc=1)

        phase1(P, nc, T1, after_st=ag1)
        phase2(P, nc, T2, SEQ, seg_out=True, after_seg=ag2)
        phase3(P, nc, T3, indirect=True)
    return nc


def kernel(x, norm0_w, conv0_w_in, conv0_w_conv, conv0_w_out, norm1_w, gla1_w_in, gla1_w_gk2,
           gla1_b_gk2, gla1_gn_w, gla1_w_out, norm_f_w):
    f = lambda a: np.asarray(a, dtype=np.float32)
    x, norm0_w, conv0_w_in, conv0_w_conv, conv0_w_out, norm1_w = map(f, (x, norm0_w, conv0_w_in, conv0_w_conv, conv0_w_out, norm1_w))
    gla1_w_in, gla1_w_gk2, gla1_b_gk2, gla1_gn_w, gla1_w_out, norm_f_w = map(f, (gla1_w_in, gla1_w_gk2, gla1_b_gk2, gla1_gn_w, gla1_w_out, norm_f_w))
    cores = list(range(NCORES))
    maps = p1_inputs(x, norm0_w, conv0_w_in, conv0_w_conv, conv0_w_out, norm1_w)
    wo1 = wblocks(gla1_w_out)
    nf = vec_pk(norm_f_w)
    pp, kk = np.meshgrid(np.arange(128), np.arange(KT), indexing="ij")
    for c in cores:
        b, hs = divmod(c, 4)
        m = maps[c]
        m["w_out0"] = m.pop("w_out")
        m.update(p2_weights(hs, gla1_w_in, gla1_w_gk2, gla1_b_gk2, gla1_gn_w))
        m["w_out1"] = wo1
        m["nfw"] = nf
        m["yidx"] = np.ascontiguousarray(np.concatenate(
            [(((hs * 2 + half) * 4 + kk % 4) * 512 + (kk // 4) * 128 + pp) for half in range(2)], axis=1).astype(np.int32))
    res = run_bass_kernel_spmd(build_fused(), maps, core_ids=cores).results
    out = np.empty((NB, SEQ, D), np.float32)
    for c in cores:
        b, s = divmod(c, 4)
        out[b, s * TOK:(s + 1) * TOK, :] = np.asarray(res[c]["oT"]).T
    return out
```
